# Optimizing a Trainium2 kernel written in Bass

```python
import math
import jax, jax.numpy as jnp
from jax import lax
import numpy as np

D_MODEL = 2048
BATCH = 4
SEQ = 8192
DEPTH = 1

S5_WIDTH = D_MODEL // 4
S5_GROUP = 16
S5_GROUPS = S5_WIDTH // S5_GROUP
S5_STATE = 64
N_DIR = 2
DT_MIN = 1e-3
DT_MAX = 1e-1
FNET_WIDTH = D_MODEL - S5_WIDTH
FNET_GROUP = 256
FNET_GROUPS = FNET_WIDTH // FNET_GROUP
IN_WIDTH = S5_WIDTH + FNET_WIDTH + 2 * D_MODEL
D_FF = -(-8 * D_MODEL // (3 * 256)) * 256
N_MOD = 6
EPS = 1e-6

kernel_name = "hybrid_s5_fnet_gated_encoder_block"


def rms_norm(x, g):
    xf = x.astype(jnp.float32)
    y = xf * lax.rsqrt(jnp.mean(xf * xf, axis=-1, keepdims=True) + EPS)
    return (y * g.astype(jnp.float32)).astype(x.dtype)


def modulate(h, shift, scale):
    return h * (1.0 + scale[:, None, :]) + shift[:, None, :]


def _complex_combine(left, right):
    a1r, a1i, b1r, b1i = left
    a2r, a2i, b2r, b2i = right
    ar = a2r * a1r - a2i * a1i
    ai = a2r * a1i + a2i * a1r
    br = a2r * b1r - a2i * b1i + b2r
    bi = a2r * b1i + a2i * b1r + b2i
    return ar, ai, br, bi


def s5_direction(u, lam_re, lam_im, log_step, b_re, b_im, c_re, c_im, reverse):
    dt = jnp.exp(log_step)[:, None]
    mag = jnp.exp(lam_re * dt)
    ang = lam_im * dt
    lb_re = mag * jnp.cos(ang)
    lb_im = mag * jnp.sin(ang)
    den = lam_re * lam_re + lam_im * lam_im
    num_re = lb_re - 1.0
    num_im = lb_im
    coef_re = (num_re * lam_re + num_im * lam_im) / den
    coef_im = (num_im * lam_re - num_re * lam_im) / den
    bb_re = coef_re[..., None] * b_re - coef_im[..., None] * b_im
    bb_im = coef_re[..., None] * b_im + coef_im[..., None] * b_re
    bu_re = jnp.einsum('bsgh,gnh->bsgn', u, bb_re)
    bu_im = jnp.einsum('bsgh,gnh->bsgn', u, bb_im)
    a_re = jnp.broadcast_to(lb_re, bu_re.shape)
    a_im = jnp.broadcast_to(lb_im, bu_im.shape)
    _, _, st_re, st_im = lax.associative_scan(
        _complex_combine, (a_re, a_im, bu_re, bu_im), axis=1, reverse=reverse)
    return (jnp.einsum('bsgn,ghn->bsgh', st_re, c_re)
            - jnp.einsum('bsgn,ghn->bsgh', st_im, c_im))


def s5_branch(u_in, lam_re, lam_im, log_step, b_re, b_im, c_re, c_im, d_skip, w_glu):
    bsz, seq, _ = u_in.shape
    u = u_in.astype(jnp.float32).reshape(bsz, seq, S5_GROUPS, S5_GROUP)
    f32 = lambda t: t.astype(jnp.float32)
    y = d_skip.astype(jnp.float32).reshape(S5_GROUPS, S5_GROUP) * u
    for d in range(N_DIR):
        y = y + s5_direction(u, f32(lam_re[d]), f32(lam_im[d]), f32(log_step[d]),
                             f32(b_re[d]), f32(b_im[d]), f32(c_re[d]), f32(c_im[d]),
                             reverse=(d == 1))
    y = jax.nn.gelu(y.reshape(bsz, seq, S5_WIDTH)).astype(u_in.dtype)
    val, gate = jnp.split(y @ w_glu, 2, axis=-1)
    return val * jax.nn.sigmoid(gate)


def fnet_branch(u_in):
    bsz, seq, _ = u_in.shape
    u = u_in.astype(jnp.float32).reshape(bsz, seq, FNET_GROUPS, FNET_GROUP)
    z = jnp.fft.fft2(u, axes=(1, 3), norm='ortho').real
    return z.reshape(bsz, seq, FNET_WIDTH).astype(u_in.dtype)


def setup_inputs(seed: int = 0) -> dict:
    key = jax.random.key(seed)
    ks = jax.random.split(key, 24)
    nrm = jax.random.normal

    def dense(k, shape, fan_in):
        return nrm(k, shape, jnp.float32) * (fan_in ** -0.5)

    L = DEPTH
    x = nrm(ks[0], (BATCH, SEQ, D_MODEL), jnp.float32)
    c = nrm(ks[1], (BATCH, D_MODEL), jnp.float32)
    w_ada = dense(ks[2], (L, D_MODEL, N_MOD * D_MODEL), D_MODEL)
    b_ada = 0.01 * nrm(ks[3], (L, N_MOD * D_MODEL), jnp.float32)
    norm_mix = 1.0 + 0.01 * nrm(ks[4], (L, D_MODEL), jnp.float32)
    w_in = dense(ks[5], (L, D_MODEL, IN_WIDTH), D_MODEL)
    sshape = (L, N_DIR, S5_GROUPS, S5_STATE)
    s5_lambda_re = -0.5 + 0.01 * nrm(ks[6], sshape, jnp.float32)
    s5_lambda_im = (jnp.pi * jnp.arange(S5_STATE, dtype=jnp.float32)
                    + 0.01 * nrm(ks[7], sshape, jnp.float32))
    s5_log_step = jax.random.uniform(ks[8], (L, N_DIR, S5_GROUPS), jnp.float32,
                                     math.log(DT_MIN), math.log(DT_MAX))
    bshape = (L, N_DIR, S5_GROUPS, S5_STATE, S5_GROUP)
    s5_b_re = dense(ks[9], bshape, 2 * S5_GROUP)
    s5_b_im = dense(ks[10], bshape, 2 * S5_GROUP)
    cshape = (L, N_DIR, S5_GROUPS, S5_GROUP, S5_STATE)
    s5_c_re = dense(ks[11], cshape, 2 * S5_STATE)
    s5_c_im = dense(ks[12], cshape, 2 * S5_STATE)
    s5_d = nrm(ks[13], (L, S5_WIDTH), jnp.float32)
    w_s5_glu = dense(ks[14], (L, S5_WIDTH, 2 * S5_WIDTH), S5_WIDTH)
    w_branch_s5 = dense(ks[15], (L, S5_WIDTH, D_MODEL), S5_WIDTH)
    w_branch_fnet = dense(ks[16], (L, FNET_WIDTH, D_MODEL), FNET_WIDTH)
    w_out = dense(ks[17], (L, D_MODEL, D_MODEL), D_MODEL)
    norm_ffn = 1.0 + 0.01 * nrm(ks[18], (L, D_MODEL), jnp.float32)
    w_ffn_in = dense(ks[19], (L, D_MODEL, 2 * D_FF), D_MODEL)
    w_ffn_out = dense(ks[20], (L, D_FF, D_MODEL), D_FF)
    norm_final = 1.0 + 0.01 * nrm(ks[21], (D_MODEL,), jnp.float32)
    return {"x": x, "c": c, "w_ada": w_ada, "b_ada": b_ada, "norm_mix": norm_mix,
            "w_in": w_in, "s5_lambda_re": s5_lambda_re, "s5_lambda_im": s5_lambda_im,
            "s5_log_step": s5_log_step, "s5_b_re": s5_b_re, "s5_b_im": s5_b_im,
            "s5_c_re": s5_c_re, "s5_c_im": s5_c_im, "s5_d": s5_d, "w_s5_glu": w_s5_glu,
            "w_branch_s5": w_branch_s5, "w_branch_fnet": w_branch_fnet, "w_out": w_out,
            "norm_ffn": norm_ffn, "w_ffn_in": w_ffn_in, "w_ffn_out": w_ffn_out,
            "norm_final": norm_final}


def reference(x, c, w_ada, b_ada, norm_mix, w_in, s5_lambda_re, s5_lambda_im, s5_log_step,
              s5_b_re, s5_b_im, s5_c_re, s5_c_im, s5_d, w_s5_glu, w_branch_s5, w_branch_fnet,
              w_out, norm_ffn, w_ffn_in, w_ffn_out, norm_final):
    c_act = jax.nn.silu(c)
    for l in range(DEPTH):
        mod = c_act @ w_ada[l] + b_ada[l]
        sh_m, sc_m, g_m, sh_f, sc_f, g_f = jnp.split(mod, N_MOD, axis=-1)

        h = modulate(rms_norm(x, norm_mix[l]), sh_m, sc_m)
        proj = h @ w_in[l]
        o1 = S5_WIDTH
        o2 = o1 + FNET_WIDTH
        o3 = o2 + D_MODEL
        y_s5 = s5_branch(proj[..., :o1], s5_lambda_re[l], s5_lambda_im[l], s5_log_step[l],
                         s5_b_re[l], s5_b_im[l], s5_c_re[l], s5_c_im[l], s5_d[l], w_s5_glu[l])
        y_fn = fnet_branch(proj[..., o1:o2])
        gate_s5 = jax.nn.sigmoid(proj[..., o2:o3])
        gate_fn = jax.nn.sigmoid(proj[..., o3:])
        merged = gate_s5 * (y_s5 @ w_branch_s5[l]) + gate_fn * (y_fn @ w_branch_fnet[l])
        x = x + g_m[:, None, :] * (merged @ w_out[l])

        h2 = modulate(rms_norm(x, norm_ffn[l]), sh_f, sc_f)
        a, b = jnp.split(h2 @ w_ffn_in[l], 2, axis=-1)
        x = x + g_f[:, None, :] * ((jax.nn.silu(a) * b) @ w_ffn_out[l])
    return rms_norm(x, norm_final)
```

```python
import contextlib
import numpy as np
import ml_dtypes
import concourse.bass as bass
import concourse.mybir as mybir
from concourse.bass_utils import run_bass_kernel_spmd

F32 = mybir.dt.float32
BF16 = mybir.dt.bfloat16
AF = mybir.ActivationFunctionType
ALU = mybir.AluOpType

D = 2048
S = 8192
OWN = 4096
DFF = 5632
EPS = 1e-6
NE = 58


class Op:
    __slots__ = ("eng", "fn", "r", "w", "dma", "deps", "signal", "ev", "semkey")

    def __init__(self, eng, fn, r, w, dma, semkey):
        self.eng = eng
        self.fn = fn
        self.r = r
        self.w = w
        self.dma = dma
        self.deps = ()
        self.signal = False
        self.ev = None
        self.semkey = semkey


class Sched:
    ENGS = ("pe", "act", "dve", "pool", "sp")

    def __init__(self):
        self.ops = []

    def add(self, eng, fn, r=(), w=(), dma=False, semkey=None, nobar=False):
        r = tuple(r)
        if not nobar:
            r = r + ("PHASE",)
        self.ops.append(Op(eng, fn, r, tuple(w), dma, semkey))

    def barrier(self, fn):
        self.ops.append(Op("pool", fn, (), ("PHASE",), False, None))

    def analyze(self):
        ops = self.ops
        last_w = {}
        rd_eng = {}
        rd_dma = {}
        for i, op in enumerate(ops):
            deps = {}
            for k in op.r:
                j = last_w.get(k)
                if j is not None:
                    deps[j] = True
            for k in op.w:
                j = last_w.get(k)
                if j is not None:
                    deps.setdefault(j, False)
                for j in rd_eng.get(k, {}).values():
                    deps.setdefault(j, False)
                for j in rd_dma.get(k, ()):
                    deps.setdefault(j, False)
            keep = []
            for j, raw in deps.items():
                if j == i:
                    continue
                oj = ops[j]
                if (not oj.dma) and (not op.dma) and oj.eng == op.eng and not raw:
                    continue
                keep.append(j)
                oj.signal = True
            op.deps = keep
            for k in op.r:
                if op.dma:
                    rd_dma.setdefault(k, []).append(i)
                else:
                    rd_eng.setdefault(k, {})[op.eng] = i
            for k in op.w:
                last_w[k] = i
                rd_eng[k] = {}
                rd_dma[k] = []

    def emit(self, nc, stack):
        self.analyze()
        ops = self.ops
        sems = {}

        def getsem(name):
            if name not in sems:
                sems[name] = stack.enter_context(nc.semaphore("s_" + str(len(sems))))
            return sems[name]

        cnt = {}
        for op in ops:
            if op.dma:
                name = ("dma", op.semkey if op.semkey is not None else op.w[0])
                cnt[name] = cnt.get(name, 0) + 16
                op.ev = (name, cnt[name])
            elif op.signal:
                name = ("eng", op.eng)
                cnt[name] = cnt.get(name, 0) + 1
                op.ev = (name, cnt[name])
        for name in cnt:
            getsem(name)
        self.nsems = len(sems)
        for sh in sems.values():
            nc.gpsimd.sem_clear(sh)
        nc.all_engine_barrier()
        block = stack.enter_context(nc.Block())

        def make_body(eng):
            def body(e):
                waited = {}
                for op in ops:
                    if op.eng != eng:
                        continue
                    need = {}
                    for j in op.deps:
                        s, v = ops[j].ev
                        if need.get(s, 0) < v:
                            need[s] = v
                    for s, v in need.items():
                        if waited.get(s, 0) < v:
                            e.wait_ge(sems[s], v)
                            waited[s] = v
                    if op.fn is not None:
                        ins = op.fn(e)
                        if op.dma:
                            ins.then_inc(sems[op.ev[0]], 16)
                        elif op.signal:
                            ins.then_inc(sems[op.ev[0]], 1)
            return body

        block.sync(make_body("sp"))
        block.scalar(make_body("act"))
        block.vector(make_body("dve"))
        block.gpsimd(make_body("pool"))
        block.tensor(make_body("pe"))


def _bf(a):
    return np.asarray(a, dtype=np.float32).astype(ml_dtypes.bfloat16)


def _const_tables(e):
    c = {}
    c["ident"] = _bf(np.eye(128))
    expo = np.zeros((128, NE), np.float32)
    for d in range(2):
        rows = slice(64 * d, 64 * d + 64)
        for tau in range(16):
            expo[rows, tau] = (15 - tau) if d == 0 else tau
            expo[rows, 16 + tau] = (tau + 1) if d == 0 else (16 - tau)
        for dl in range(2):
            for t in range(8):
                expo[rows, 32 + 8 * dl + t] = (8 * dl + t) if d == 0 else (8 * dl + 7 - t)
        for k in range(8):
            expo[rows, 48 + k] = (-k) if d == 0 else (k - 7)
        expo[rows, 56] = 16
        expo[rows, 57] = 1
    c["expo"] = expo
    kk = np.arange(128) // 16
    mf = (kk[None, :] >= kk[:, None]).astype(np.float32)
    mb = (kk[:, None] >= kk[None, :]).astype(np.float32)
    c["cidx"] = np.tile(np.arange(512, dtype=np.float32)[None, :], (128, 1))
    c["maskf"] = mf
    c["maskb"] = mb
    s2l = np.arange(128)
    k2l = np.arange(64)
    s2 = s2l if e == 0 else 127 - s2l
    k2 = k2l if e == 0 else 127 - k2l
    ph = 2 * np.pi * (s2[:, None].astype(np.float64) * k2[None, :]) / 128.0
    sa = 2.0 ** -5
    c["fa"] = _bf(np.concatenate([sa * np.cos(ph), -sa * np.sin(ph)], axis=1))
    s1l = np.arange(64)
    k1l = np.arange(64)
    s1 = s1l if e == 0 else 63 - s1l
    k1 = k1l if e == 0 else 63 - k1l
    gb = np.zeros((128, 32, 2, 2, 2, 64), np.float64)
    for j in range(32):
        for b in range(2):
            k2a = k2[2 * j + b]
            phi = 2 * np.pi * (s1[:, None] * k1[None, :] / 64.0 + s1[:, None] * float(k2a) / 8192.0)
            gr = sa * np.cos(phi)
            gi = -sa * np.sin(phi)
            rows = slice(64 * b, 64 * b + 64)
            gb[rows, j, 0, 0, b, :] = gr
            gb[rows, j, 0, 1, b, :] = gi
            gb[rows, j, 1, 0, b, :] = -gi
            gb[rows, j, 1, 1, b, :] = gr
    c["fg"] = _bf(gb.reshape(128, 32 * 2 * 256))
    ch = np.arange(256)
    kc = np.arange(256)
    phc = 2 * np.pi * (ch[:, None].astype(np.float64) * kc[None, :]) / 256.0
    sc = 2.0 ** -0.5
    ct = np.zeros((128, 2, 2, 256), np.float64)
    for cb in range(2):
        ct[:, cb, 0, :] = sc * np.cos(phc[128 * cb:128 * cb + 128])
        ct[:, cb, 1, :] = sc * np.sin(phc[128 * cb:128 * cb + 128])
    c["fc"] = _bf(ct.reshape(128, 1024))
    return c


def _prep_core(inp, b, e, consts):
    m = {}
    x = inp["x"][b]
    m["xs"] = np.ascontiguousarray(x if e == 0 else x[::-1])
    m["cvec"] = np.ascontiguousarray(inp["c"][b].reshape(16, 128).T)
    m["b_adaT"] = np.ascontiguousarray(inp["b_ada"][0].reshape(96, 128).T)
    m["b_ada_row"] = np.ascontiguousarray(inp["b_ada"][0].reshape(1, 12288))
    m["nmixT"] = np.ascontiguousarray(inp["norm_mix"][0].reshape(16, 128).T)
    m["nffnT"] = np.ascontiguousarray(inp["norm_ffn"][0].reshape(16, 128).T)
    m["nfin_row"] = np.ascontiguousarray(inp["norm_final"].reshape(1, D))
    dsel = [0, 1] if e == 0 else [1, 0]

    def dn_g(a):
        return np.ascontiguousarray(a[dsel].transpose(0, 2, 1).reshape(128, 32))

    m["lamre"] = dn_g(inp["s5_lambda_re"][0])
    m["lamim"] = dn_g(inp["s5_lambda_im"][0])
    ls = inp["s5_log_step"][0][dsel]
    m["lstep"] = np.ascontiguousarray(np.repeat(ls[:, None, :], 64, axis=1).reshape(128, 32))
    m["bre"] = np.ascontiguousarray(inp["s5_b_re"][0][dsel].transpose(0, 2, 1, 3).reshape(128, 512))
    m["bim"] = np.ascontiguousarray(inp["s5_b_im"][0][dsel].transpose(0, 2, 1, 3).reshape(128, 512))
    m["cre"] = np.ascontiguousarray(inp["s5_c_re"][0][dsel].transpose(0, 3, 1, 2).reshape(128, 512))
    m["cim"] = np.ascontiguousarray(inp["s5_c_im"][0][dsel].transpose(0, 3, 1, 2).reshape(128, 512))
    dsk = inp["s5_d"][0].reshape(32, 16)
    m["dskip"] = np.ascontiguousarray(np.tile(dsk.T[None, :, :], (8, 1, 1)).reshape(128, 32))
    for k in ("w_ada", "w_in", "w_s5_glu", "w_branch_s5", "w_branch_fnet", "w_out", "w_ffn_in", "w_ffn_out"):
        m[k] = inp[k][0]
    for k, v in consts.items():
        m["k_" + k] = v
    return m


SB_WORDS = 48640
PBASE = 4096


class Region:
    def __init__(self, big, base):
        self.big = big
        self.off = base

    def alloc(self, shape, dt):
        es = 4 if dt == F32 else 2
        n = 1
        for s in shape:
            n *= s
        nbytes = (n * es + 31) // 32 * 32
        off = self.off
        self.off += nbytes
        assert self.off <= SB_WORDS * 4, ("sbuf overflow", self.off)
        ap = self.big[:, off // 4:(off + nbytes) // 4]
        if dt != F32:
            ap = ap.bitcast(dt)
        ap = ap[:, 0:n]
        if len(shape) == 2:
            ap = ap.rearrange("p (a b) -> p a b", a=shape[0], b=shape[1])
        elif len(shape) == 3:
            ap = ap.rearrange("p (a b c) -> p a b c", a=shape[0], b=shape[1], c=shape[2])
        elif len(shape) == 4:
            ap = ap.rearrange("p (a b c d) -> p a b c d", a=shape[0], b=shape[1], c=shape[2], d=shape[3])
        return ap


def build_nc(phases=("p0", "p1", "s5", "fn", "p4"), dbg=()):
    nc = bass.Bass("TRN2", target_bir_lowering=False)
    stack = contextlib.ExitStack()
    sch = Sched()

    def din(name, shape, dt=F32):
        return nc.dram_tensor(name, list(shape), dt, kind="ExternalInput").ap()

    def dscr(name, shape, dt, out=False):
        if out or name in dbg:
            return nc.dram_tensor(name, list(shape), dt, kind="ExternalOutput").ap()
        return nc.dram_tensor(name, list(shape), dt).ap()

    xs = din("xs", [S, D])
    cvec = din("cvec", [128, 16])
    b_adaT = din("b_adaT", [128, 96])
    b_ada_row = din("b_ada_row", [1, 12288])
    nmixT = din("nmixT", [128, 16])
    nffnT = din("nffnT", [128, 16])
    nfin_row = din("nfin_row", [1, D])
    lamre = din("lamre", [128, 32])
    lamim = din("lamim", [128, 32])
    lstep = din("lstep", [128, 32])
    bre = din("bre", [128, 512])
    bim = din("bim", [128, 512])
    cre = din("cre", [128, 512])
    cim = din("cim", [128, 512])
    dskip = din("dskip", [128, 32])
    w_ada = din("w_ada", [D, 12288])
    w_in = din("w_in", [D, 6144])
    w_glu = din("w_s5_glu", [512, 1024])
    w_bs = din("w_branch_s5", [512, D])
    w_bf = din("w_branch_fnet", [1536, D])
    w_out = din("w_out", [D, D])
    w_f1 = din("w_ffn_in", [D, 2 * DFF])
    w_f2 = din("w_ffn_out", [DFF, D])
    k_ident = din("k_ident", [128, 128], BF16)
    k_expo = din("k_expo", [128, NE])
    k_maskf = din("k_maskf", [128, 128])
    k_cidx = din("k_cidx", [128, 512])
    k_maskb = din("k_maskb", [128, 128])
    k_fa = din("k_fa", [128, 128], BF16)
    k_fg = din("k_fg", [128, 32 * 512], BF16)
    k_fc = din("k_fc", [128, 1024], BF16)

    yo = dscr("yo", [OWN, D], F32, out=True)
    wb_gate = dscr("wb_gate", [D, 4096], BF16)
    wb_glu = dscr("wb_glu", [512, 1024], BF16)
    wb_bs = dscr("wb_bs", [512, D], BF16)
    wb_bf = dscr("wb_bf", [1536, D], BF16)
    wb_out = dscr("wb_out", [D, D], BF16)
    wb_f1 = dscr("wb_f1", [D, 2 * DFF], BF16)
    wb_f2 = dscr("wb_f2", [DFF, D], BF16)
    Ufn = dscr("Ufn", [S, 1536], BF16)
    Us = dscr("Us", [32, 128, 1024], BF16)
    Yd = dscr("Yd", [128, 64, 1536], BF16)
    Yfn = dscr("Yfn", [1536, OWN], BF16)
    Yg = dscr("Yg", [512, OWN], BF16)
    modrow = dscr("modrow", [2, D], F32)
    dbg_modT = dscr("dbg_modT", [128, 96], F32) if "dbg_modT" in dbg else None
    dbg_x1 = dscr("dbg_x1", [OWN, D], F32) if "dbg_x1" in dbg else None
    dbg_mg = dscr("dbg_mg", [128, 24, OWN], BF16) if "dbg_mg" in dbg else None

    big = stack.enter_context(nc.sbuf_tensor("big", [128, SB_WORDS], F32))
    psf = [stack.enter_context(nc.psum_tensor("ps%d" % i, [128, 512], F32)) for i in range(8)]
    psb = [p[:, :].bitcast(BF16) for p in psf]
    pstate = {"i": 0}

    def pb():
        i = pstate["i"]
        pstate["i"] = (i + 1) % 8
        return i

    PR = Region(big, 0)
    IDENT = PR.alloc([128], BF16)
    ONES = PR.alloc([128], F32)
    MODT = PR.alloc([96], F32)
    A1 = PR.alloc([16], F32)
    A2 = PR.alloc([16], F32)
    CV = PR.alloc([16], F32)
    CACT = PR.alloc([16], F32)
    NMT = PR.alloc([16], F32)
    NFT = PR.alloc([16], F32)
    BAT = PR.alloc([96], F32)
    SS = PR.alloc([16], F32)
    SD = PR.alloc([16], F32)
    RS = PR.alloc([16], F32)
    DUMMY = PR.alloc([8], F32)
    assert PR.off <= PBASE
    B1 = MODT[:, 0:16]
    B2 = MODT[:, 48:64]

    def dma(eng, out, in_, r, wkey, semkey=None, nobar=False):
        sch.add(eng, lambda e: e.dma_start(out=out, in_=in_), r=r, w=(wkey,), dma=True, semkey=semkey, nobar=nobar)

    def dma_multi(eng, out, in_, r, wkeys, semkey):
        sch.add(eng, lambda e: e.dma_start(out=out, in_=in_), r=r, w=tuple(wkeys), dma=True, semkey=semkey)

    def barrier():
        sch.barrier(lambda e: e.memset(DUMMY[:, 0:1], 0.0))

    def cast_weight(dst, src, rows, key, c0=0, ncols=None):
        ncols = ncols if ncols is not None else dst.shape[1]
        for r0 in range(0, rows, 128):
            dma("pool", dst[r0:r0 + 128, :], src[r0:r0 + 128, c0:c0 + ncols], (), key, nobar=True)

    dma("sp", IDENT, k_ident[:, :], (), "ident")
    dma("sp", CV, cvec[:, :], (), "cv")
    dma("sp", BAT, b_adaT[:, :], (), "bat")
    dma("sp", NMT, nmixT[:, :], (), "nmt")
    dma("sp", NFT, nffnT[:, :], (), "nft")
    sch.add("pool", lambda e: e.memset(ONES, 1.0), w=("ones",))
    sch.add("act", lambda e: e.activation(out=CACT, in_=CV, func=AF.Silu), r=("cv",), w=("cact",))
    R1 = Region(big, PBASE)
    JUNK = R1.alloc([2048], BF16)
    WIN = R1.alloc([16, 2048], BF16)
    if "p1" in phases:
        w_in_v = w_in.rearrange("(kc p) n -> p kc n", p=128)
        for q in range(4):
            dma("pool", WIN[:, 4 * q:4 * q + 4, :], w_in_v[:, 4 * q:4 * q + 4, 0:2048], (), ("win", q))
    if "p0" in phases:
        R0 = Region(big, PBASE + 80 * 1024)
        ACC = R0.alloc([4096], F32)
        SLAB = [R0.alloc([4096], F32) for _ in range(3)]
        BROW = R0.alloc([2048], F32)
        ROWST = R0.alloc([2048], F32)
        si = 0
        for part in range(3):
            for kc in range(16):
                sl = si % 3
                si += 1
                dma("sp", SLAB[sl], w_ada[128 * kc:128 * kc + 128, 4096 * part:4096 * part + 4096], (), ("wa", sl))
                if kc == 0:
                    sch.add("dve", lambda e, sl=sl, kc=kc: e.tensor_scalar(ACC, SLAB[sl], CACT[:, kc:kc + 1], None, op0=ALU.mult),
                            r=(("wa", sl), "cact"), w=("acc",))
                else:
                    sch.add("dve", lambda e, sl=sl, kc=kc: e.scalar_tensor_tensor(out=ACC, in0=SLAB[sl], scalar=CACT[:, kc:kc + 1], in1=ACC,
                                                                               op0=ALU.mult, op1=ALU.add),
                            r=(("wa", sl), "cact", "acc"), w=("acc",))
            if 'nope' in dbg:
                continue
            bk = pb()
            for j in range(32):
                sch.add("pe", lambda e, j=j, bk=bk: e.matmul(psf[bk][:, j:j + 1], lhsT=ACC[:, 128 * j:128 * j + 128], rhs=ONES[:, 0:1],
                                                             start=True, stop=True),
                        r=("acc", "ones"), w=(("ps", bk),))
            sch.add("dve", lambda e, bk=bk, part=part: e.tensor_tensor(out=MODT[:, 32 * part:32 * part + 32], in0=psf[bk][:, 0:32],
                                                                        in1=BAT[:, 32 * part:32 * part + 32], op=ALU.add),
                    r=(("ps", bk), "bat"), w=("modt",))
            if part >= 1 and 'norow' not in dbg:
                coff = 0 if part == 1 else 2048
                gcol = 4096 if part == 1 else 10240
                dma("sp", BROW, b_ada_row[0:1, gcol:gcol + 2048].partition_broadcast(128), (), "brow")
                for nb in range(4):
                    bk = pb()
                    sch.add("pe", lambda e, bk=bk, nb=nb, coff=coff: e.matmul(psf[bk][:, :], lhsT=ONES, rhs=ACC[:, coff + 512 * nb:coff + 512 * nb + 512],
                                                                             start=True, stop=True),
                            r=("acc", "ones"), w=(("ps", bk),))
                    sch.add("dve", lambda e, bk=bk, nb=nb: e.tensor_tensor(out=ROWST[:, 512 * nb:512 * nb + 512], in0=psf[bk][:, :],
                                                                           in1=BROW[:, 512 * nb:512 * nb + 512], op=ALU.add),
                            r=(("ps", bk), "brow"), w=("rowst",))
                dma("sp", modrow[part - 1:part, :], ROWST[0:1, :], ("rowst",), "modrow")
        sch.add("dve", lambda e: e.scalar_tensor_tensor(out=A1, in0=MODT[:, 16:32], scalar=1.0, in1=NMT, op0=ALU.add, op1=ALU.mult),
                r=("modt", "nmt"), w=("a1",))
        sch.add("dve", lambda e: e.scalar_tensor_tensor(out=A2, in0=MODT[:, 64:80], scalar=1.0, in1=NFT, op0=ALU.add, op1=ALU.mult),
                r=("modt", "nft"), w=("a2",))
        if dbg_modT is not None:
            dma("sp", dbg_modT[:, :], MODT, ("modt",), "dbg_modT")
        barrier()

    def rms_block(xt_ap, xkey, xn_ap, xnkey, col):
        c1 = slice(col, col + 1)
        sch.add("act", lambda e: e.activation(out=JUNK, in_=xt_ap, func=AF.Square, accum_out=SS[:, c1]),
                r=(xkey,), w=("junk", ("ss", col)))
        sch.add("act", lambda e: e.activation(out=SD[:, c1], in_=SS[:, c1], func=AF.Sqrt, scale=1.0 / D, bias=EPSB[:, 0:1]),
                r=(("ss", col), "epsb"), w=(("sd", col),))
        sch.add("dve", lambda e: e.reciprocal(RS[:, c1], SD[:, c1]), r=(("sd", col),), w=(("rs", col),))
        sch.add("dve", lambda e: e.tensor_scalar(xn_ap, xt_ap, RS[:, c1], None, op0=ALU.mult),
                r=(xkey, ("rs", col)), w=(xnkey,))

    def transpose_mod(xn_ap, xnkey, ht_ap, htkey, r, Acol, Bcol, akey, bkey, cnt):
        for cg in range(2):
            bk = pb()
            for c8 in range(8):
                ch = 8 * cg + c8
                sch.add("pe", lambda e, bk=bk, c8=c8, ch=ch: e.transpose(psb[bk][:, 128 * c8:128 * c8 + 128], xn_ap[:, 128 * ch:128 * ch + 128], IDENT),
                        r=(xnkey, "ident"), w=(("ps", bk),))
            for c8 in range(8):
                ch = 8 * cg + c8
                src = psb[bk][:, 128 * c8:128 * c8 + 128]
                dst = ht_ap[:, ch, 128 * r:128 * r + 128]
                if cg == 0:
                    sch.add("act", lambda e, src=src, dst=dst, ch=ch: e.activation(out=dst, in_=src, func=AF.Identity, scale=Acol[:, ch:ch + 1], bias=Bcol[:, ch:ch + 1]),
                            r=(("ps", bk), akey, bkey), w=(htkey,))
                else:
                    sch.add("dve", lambda e, src=src, dst=dst, ch=ch: e.tensor_scalar(dst, src, Acol[:, ch:ch + 1], Bcol[:, ch:ch + 1], op0=ALU.mult, op1=ALU.add),
                            r=(("ps", bk), akey, bkey), w=(htkey,))
                cnt[0] += 1

    EPSB = PR.alloc([8], F32)
    sch.add("pool", lambda e: e.memset(EPSB, EPS), w=("epsb",))
    if "p1" in phases:
        XT = [R1.alloc([2048], F32) for _ in range(4)]
        XN = [R1.alloc([2048], BF16) for _ in range(4)]
        HT = [R1.alloc([16, 512], BF16) for _ in range(2)]
        UFN = [R1.alloc([1536], BF16) for _ in range(2)]
        US5 = R1.alloc([4, 8, 256], BF16)
        winkeys = tuple(("win", q) for q in range(4))
        if "p4" in phases:
            cast_weight(wb_gate, w_in, D, "wb_gate", c0=2048, ncols=4096)
            cast_weight(wb_glu, w_glu, 512, "wb_glu")
            cast_weight(wb_bs, w_bs, 512, "wb_bs")
            cast_weight(wb_bf, w_bf, 1536, "wb_bf")
            cast_weight(wb_out, w_out, D, "wb_out")
            cast_weight(wb_f1, w_f1, D, "wb_f1")
            cast_weight(wb_f2, w_f2, DFF, "wb_f2")
        ecnt = [0]
        def p1_rms(tt):
            for r in range(4):
                dma("act", XT[r], xs[512 * tt + 128 * r:512 * tt + 128 * r + 128, :], (), ("xt", r))
                rms_block(XT[r], ("xt", r), XN[r], ("xn", r), (4 * tt + r) % 8)

        p1_rms(0)

        def p1_fnet(tt, hs, r):
            us = (4 * tt + r) % 2
            bks = [pb() for _ in range(3)]
            for kc in range(16):
                for nb in range(3):
                    sch.add("pe", lambda e, kc=kc, nb=nb, r=r, hs=hs, bk=bks[nb]: e.matmul(psf[bk][:, :], lhsT=HT[hs][:, kc, 128 * r:128 * r + 128],
                                                                                            rhs=WIN[:, kc, 512 + 512 * nb:1024 + 512 * nb],
                                                                                            start=(kc == 0), stop=(kc == 15)),
                            r=(("ht", hs, r),) + winkeys, w=(("ps", bks[nb]),))
            for nb in range(3):
                if nb == 1:
                    sch.add("act", lambda e, nb=nb, us=us, bk=bks[nb]: e.activation(out=UFN[us][:, 512 * nb:512 * nb + 512], in_=psf[bk][:, :], func=AF.Copy),
                            r=(("ps", bks[nb]),), w=(("ufn", us),))
                else:
                    sch.add("dve", lambda e, nb=nb, us=us, bk=bks[nb]: e.tensor_copy(UFN[us][:, 512 * nb:512 * nb + 512], psf[bk][:, :]),
                            r=(("ps", bks[nb]),), w=(("ufn", us),))
            row0 = 512 * tt + 128 * r
            dma("sp", Ufn[row0:row0 + 128, :], UFN[us], (("ufn", us),), "Ufn")

        for tt in range(S // 512):
            hs = tt % 2
            for r in range(4):
                transpose_mod(XN[r], ("xn", r), HT[hs], ("ht", hs, r), r, A1, B1, "a1", "modt", ecnt)
                if r >= 1:
                    p1_fnet(tt, hs, r - 1)
            if tt + 1 < S // 512:
                p1_rms(tt + 1)
            p1_fnet(tt, hs, 3)
            hkeys = tuple(("ht", hs, r) for r in range(4))
            for blk in range(4):
                bk = pb()
                for kc in range(16):
                    sch.add("pe", lambda e, bk=bk, kc=kc, blk=blk, hs=hs: e.matmul(psf[bk][:, :], lhsT=WIN[:, kc, 128 * blk:128 * blk + 128], rhs=HT[hs][:, kc, :],
                                                                                   start=(kc == 0), stop=(kc == 15)),
                            r=hkeys + winkeys, w=(("ps", bk),))
                j0 = 64 * (tt % 4)
                sch.add("act", lambda e, bk=bk, blk=blk, j0=j0: e.activation(out=US5[:, blk, :, j0:j0 + 64], in_=psf[bk][:, :].rearrange("p (j k) -> p k j", k=8),
                                                                             func=AF.Copy),
                        r=(("ps", bk),), w=(("us5", blk),))
            if tt % 4 == 3:
                rnd = tt // 4
                for blk in range(4):
                    for gl in range(8):
                        g = 8 * blk + gl
                        dma("sp", Us[g].rearrange("(k h) j -> h k j", h=16)[:, :, 256 * rnd:256 * rnd + 256],
                            US5[16 * gl:16 * gl + 16, blk, :, :], (("us5", blk),), "Us")
        barrier()

    I32 = mybir.dt.int32
    KB = 1024
    base0 = PBASE + 4 * KB

    def RG(off_kb):
        return Region(big, base0 + off_kb * KB)

    def s5_half(gh):
        GE = "dve"
        NG = 16
        g0 = NG * gh
        SP_ = RG(0)
        LR, LI, LS, DT, LRD, TH, DSK = [SP_.alloc([NG], F32) for _ in range(7)]
        t32 = [SP_.alloc([NG], F32) for _ in range(7)]
        BR, BI, CR, CI, BBR, BBI = [SP_.alloc([NG, 16], F32) for _ in range(6)]
        EXPO = SP_.alloc([NE], F32)
        AR2, AI2, S0, S1, TA, TB, TC = [SP_.alloc([2, NG], F32) for _ in range(7)]
        SST = [S0, S1]
        assert SP_.off <= base0 + 8 * KB, SP_.off
        E_ = RG(8)
        MAG, ANG, SCY, SCIf, SCF, ER, EI = [E_.alloc([NG, NE], F32) for _ in range(7)]
        SCI = SCIf.bitcast(I32)
        assert E_.off <= base0 + 34 * KB
        SQ = RG(8).alloc([2, NG, 256], BF16)
        T_ = RG(34)
        T1 = T_.alloc([8, 16, 16], F32)
        T2 = T_.alloc([8, 16, 16], F32)
        P_ = RG(50)
        PBR = P_.alloc([NG, 16, 16], BF16)
        PBI = P_.alloc([NG, 16, 16], BF16)
        PNM = [[P_.alloc([NG, 8, 16], BF16) for _ in range(2)] for _ in range(2)]
        assert P_.off <= base0 + 82 * KB
        TAB = RG(82).alloc([NG, 7, 128], BF16)
        ZB = RG(110).alloc([2, NG, 513], BF16)
        UALL = RG(143).alloc([NG, 1024], BF16)
        Y_ = RG(175)
        YGS = [Y_.alloc([512], BF16) for _ in range(2)]
        MASKF = Y_.alloc([128], F32)
        MASKB = Y_.alloc([128], F32)
        TMPA = Y_.alloc([128], F32)
        TMPB = Y_.alloc([128], F32)
        IDENTF = Y_.alloc([128], F32)
        RHO = Y_.alloc([NG], F32)
        PHI = Y_.alloc([NG], F32)
        PHY = Y_.alloc([NG], F32)
        PHYI = Y_.alloc([NG], F32).bitcast(I32)
        CIDX = RG(8).alloc([512], F32) if False else None
        COSB = RG(24).alloc([NG, 512], BF16)
        SINB = RG(40).alloc([NG, 512], BF16)
        WK = [RG(56 + 8 * i_).alloc([4, 512], F32) for i_ in range(3)]
        WKI = WK[2].bitcast(I32)
        CIDX = Y_.alloc([512], F32)
        assert Y_.off <= SB_WORDS * 4
        sch.add("dve", lambda e: e.tensor_copy(IDENTF, IDENT), r=("ident",), w=("identf",))
        gs = slice(g0, g0 + NG)
        for (dst, src, k) in ((LR, lamre, "lr"), (LI, lamim, "li"), (LS, lstep, "ls"), (DSK, dskip, "dsk")):
            dma("sp", dst, src[:, gs], (), k)
        dma("sp", EXPO, k_expo[:, :], (), "expo")
        dma("sp", MASKF, k_maskf[:, :], (), "maskf")
        dma("sp", MASKB, k_maskb[:, :], (), "maskb")
        hs_ = slice(16 * g0, 16 * g0 + 16 * NG)
        dma("sp", BR.rearrange("p g h -> p (g h)"), bre[:, hs_], (), "br")
        dma("sp", BI.rearrange("p g h -> p (g h)"), bim[:, hs_], (), "bi")
        dma("sp", CR.rearrange("p g h -> p (g h)"), cre[:, hs_], (), "cr")
        dma("sp", CI.rearrange("p g h -> p (g h)"), cim[:, hs_], (), "ci")
        dma("sp", UALL, Us[gs].rearrange("g p j -> p g j"), ("Us",), "uall")
        sch.add("act", lambda e: e.activation(out=DT, in_=LS, func=AF.Exp), r=("ls",), w=("dt",))
        sch.add(GE, lambda e: e.tensor_tensor(out=LRD, in0=LR, in1=DT, op=ALU.mult), r=("lr", "dt"), w=("lrd",))
        sch.add(GE, lambda e: e.tensor_tensor(out=TH, in0=LI, in1=DT, op=ALU.mult), r=("li", "dt"), w=("th",))

        def bc_g(x, n):
            return x.unsqueeze(2).broadcast_to([128, NG, n])

        def bc_e(x, n):
            return x.unsqueeze(1).broadcast_to([128, NG, n])

        fl = lambda a: a.rearrange("p g j -> p (g j)")
        sch.add(GE, lambda e: e.tensor_tensor(out=MAG, in0=bc_g(LRD, NE), in1=bc_e(EXPO, NE), op=ALU.mult), r=("lrd", "expo"), w=("mag",))
        sch.add("act", lambda e: e.activation(out=MAG, in_=MAG, func=AF.Exp), r=("mag",), w=("mag",))
        sch.add(GE, lambda e: e.tensor_tensor(out=ANG, in0=bc_g(TH, NE), in1=bc_e(EXPO, NE), op=ALU.mult), r=("th", "expo"), w=("ang",))
        for (OUT, off, okey) in ((EI, 0.0, "ei"), (ER, 0.25, "er")):
            sch.add(GE, lambda e, off=off: e.tensor_scalar(SCY, ANG, 1.0 / (2 * np.pi), off, op0=ALU.mult, op1=ALU.add), r=("ang",), w=("scy",))
            sch.add(GE, lambda e: e.tensor_copy(SCI, SCY), r=("scy",), w=("sci",))
            sch.add(GE, lambda e: e.tensor_copy(SCF, SCI), r=("sci",), w=("scf",))
            sch.add(GE, lambda e: e.tensor_tensor(out=SCY, in0=SCY, in1=SCF, op=ALU.subtract), r=("scy", "scf"), w=("scy",))
            sch.add("act", lambda e, OUT=OUT: e.activation(out=OUT, in_=SCY, func=AF.Sin, scale=float(2 * np.pi * (1.0 - 2e-6))), r=("scy",), w=(okey,))
            sch.add(GE, lambda e, OUT=OUT: e.tensor_tensor(out=OUT, in0=OUT, in1=MAG, op=ALU.mult), r=(okey, "mag"), w=(okey,))
        numr, den, cr_, ci_, x1, x2, rden = t32
        a1r = ER[:, :, 57]
        a1i = EI[:, :, 57]
        sch.add(GE, lambda e: e.tensor_scalar(numr, a1r, -1.0, None, op0=ALU.add), r=("er",), w=("numr",))
        sch.add(GE, lambda e: e.tensor_tensor(out=x1, in0=LR, in1=LR, op=ALU.mult), r=("lr",), w=("x1",))
        sch.add(GE, lambda e: e.tensor_tensor(out=x2, in0=LI, in1=LI, op=ALU.mult), r=("li",), w=("x2",))
        sch.add(GE, lambda e: e.tensor_tensor(out=den, in0=x1, in1=x2, op=ALU.add), r=("x1", "x2"), w=("den",))
        sch.add(GE, lambda e: e.reciprocal(rden, den), r=("den",), w=("rden",))
        sch.add(GE, lambda e: e.tensor_tensor(out=x1, in0=numr, in1=LR, op=ALU.mult), r=("numr", "lr", "den"), w=("x1",))
        sch.add(GE, lambda e: e.tensor_tensor(out=x2, in0=a1i, in1=LI, op=ALU.mult), r=("ei", "li", "den"), w=("x2",))
        sch.add(GE, lambda e: e.tensor_tensor(out=cr_, in0=x1, in1=x2, op=ALU.add), r=("x1", "x2"), w=("cr_",))
        sch.add(GE, lambda e: e.tensor_tensor(out=cr_, in0=cr_, in1=rden, op=ALU.mult), r=("cr_", "rden"), w=("cr_",))
        sch.add(GE, lambda e: e.tensor_tensor(out=x1, in0=a1i, in1=LR, op=ALU.mult), r=("ei", "lr", "cr_"), w=("x1",))
        sch.add(GE, lambda e: e.tensor_tensor(out=x2, in0=numr, in1=LI, op=ALU.mult), r=("numr", "li", "cr_"), w=("x2",))
        sch.add(GE, lambda e: e.tensor_tensor(out=ci_, in0=x1, in1=x2, op=ALU.subtract), r=("x1", "x2"), w=("ci_",))
        sch.add(GE, lambda e: e.tensor_tensor(out=ci_, in0=ci_, in1=rden, op=ALU.mult), r=("ci_", "rden"), w=("ci_",))
        B1s = T1.rearrange("p a b c -> p (a b c)")[:, 0:256].rearrange("p (g h) -> p g h", h=16)
        B2s = T2.rearrange("p a b c -> p (a b c)")[:, 0:256].rearrange("p (g h) -> p g h", h=16)
        sch.add(GE, lambda e: e.tensor_tensor(out=B1s, in0=BR, in1=bc_g(cr_, 16), op=ALU.mult), r=("br", "cr_"), w=("t1",))
        sch.add(GE, lambda e: e.tensor_tensor(out=B2s, in0=BI, in1=bc_g(ci_, 16), op=ALU.mult), r=("bi", "ci_"), w=("t2",))
        sch.add(GE, lambda e: e.tensor_tensor(out=BBR, in0=B1s, in1=B2s, op=ALU.subtract), r=("t1", "t2"), w=("bbr",))
        sch.add(GE, lambda e: e.tensor_tensor(out=B1s, in0=BI, in1=bc_g(cr_, 16), op=ALU.mult), r=("bi", "cr_", "bbr"), w=("t1",))
        sch.add(GE, lambda e: e.tensor_tensor(out=B2s, in0=BR, in1=bc_g(ci_, 16), op=ALU.mult), r=("br", "ci_", "bbr"), w=("t2",))
        sch.add(GE, lambda e: e.tensor_tensor(out=BBI, in0=B1s, in1=B2s, op=ALU.add), r=("t1", "t2"), w=("bbi",))
        sch.add(GE, lambda e: e.tensor_copy(AR2[:, 0, :], ER[:, :, 56]), r=("er",), w=("ar2",))
        sch.add(GE, lambda e: e.tensor_copy(AR2[:, 1, :], ER[:, :, 56]), r=("er",), w=("ar2",))
        sch.add(GE, lambda e: e.tensor_scalar(AI2[:, 0, :], EI[:, :, 56], -1.0, None, op0=ALU.mult), r=("ei",), w=("ai2",))
        sch.add(GE, lambda e: e.tensor_copy(AI2[:, 1, :], EI[:, :, 56]), r=("ei",), w=("ai2",))
        sch.add(GE, lambda e: e.tensor_copy(RHO, MAG[:, :, 56]), r=("mag",), w=("rho",))
        sch.add(GE, lambda e: e.tensor_scalar(PHY, TH, 16.0 / (2 * np.pi), None, op0=ALU.mult), r=("th",), w=("phy",))
        sch.add(GE, lambda e: e.tensor_copy(PHYI, PHY), r=("phy",), w=("phyi",))
        sch.add(GE, lambda e: e.tensor_tensor(out=PHY, in0=PHY, in1=PHYI, op=ALU.subtract), r=("phy", "phyi"), w=("phy",))
        sch.add(GE, lambda e: e.tensor_scalar(PHI, PHY, float(2 * np.pi), None, op0=ALU.mult), r=("phy",), w=("phi",))

        def outer(OUTR, OUTI, e0, XR, XI, xkr, xki, nt, okr_, oki_, neg_im=False):
            for q in range(2):
                gq = slice(8 * q, 8 * q + 8)
                okr = okr_(q) if callable(okr_) else (okr_,)
                oki = oki_(q) if callable(oki_) else (oki_,)
                er = ER[:, gq, e0:e0 + nt].unsqueeze(3).broadcast_to([128, 8, nt, 16])
                ei = EI[:, gq, e0:e0 + nt].unsqueeze(3).broadcast_to([128, 8, nt, 16])
                xr = XR[:, gq, :].unsqueeze(2).broadcast_to([128, 8, nt, 16])
                xi = XI[:, gq, :].unsqueeze(2).broadcast_to([128, 8, nt, 16])
                t1 = T1[:, :, 0:nt, :]
                t2 = T2[:, :, 0:nt, :]
                outr = OUTR[:, gq]
                outi = OUTI[:, gq]
                sch.add(GE, lambda e, t1=t1, er=er, xr=xr: e.tensor_tensor(out=t1, in0=er, in1=xr, op=ALU.mult), r=("er", xkr), w=("t1",))
                sch.add(GE, lambda e, t2=t2, ei=ei, xi=xi: e.tensor_tensor(out=t2, in0=ei, in1=xi, op=ALU.mult), r=("ei", xki), w=("t2",))
                sch.add(GE, lambda e, t1=t1, t2=t2, outr=outr: e.tensor_tensor(out=outr, in0=t1, in1=t2, op=ALU.subtract), r=("t1", "t2"), w=okr)
                sch.add(GE, lambda e, t1=t1, er=er, xi=xi: e.tensor_tensor(out=t1, in0=er, in1=xi, op=ALU.mult), r=("er", xki) + okr, w=("t1",))
                sch.add(GE, lambda e, t2=t2, ei=ei, xr=xr: e.tensor_tensor(out=t2, in0=ei, in1=xr, op=ALU.mult), r=("ei", xkr) + okr, w=("t2",))
                if neg_im:
                    sch.add(GE, lambda e, t1=t1, t2=t2, outi=outi: e.scalar_tensor_tensor(out=outi, in0=t1, scalar=-1.0, in1=t2, op0=ALU.mult, op1=ALU.subtract),
                            r=("t1", "t2"), w=oki)
                else:
                    sch.add(GE, lambda e, t1=t1, t2=t2, outi=outi: e.tensor_tensor(out=outi, in0=t1, in1=t2, op=ALU.add), r=("t1", "t2"), w=oki)

        outer(PBR, PBI, 0, BBR, BBI, "bbr", "bbi", 16, "pbr", "pbi")
        tk = lambda g: tuple(("tab", g, s_) for s_ in range(4))
        for g in range(NG):
            if g % 2 == 0:
                bk = pb()
            for ri in range(2):
                src = PBR if ri == 0 else PBI
                for i in range(2):
                    sl = 4 * (g % 2) + 2 * ri + i
                    sch.add("pe", lambda e, bk=bk, sl=sl, src=src, g=g, i=i: e.transpose(psb[bk][:, 128 * sl:128 * sl + 128],
                                                                                         src[:, g, 8 * i:8 * i + 8, :].rearrange("p k h -> p (k h)"), IDENT),
                            r=("pbr", "pbi", "ident"), w=(("ps", bk),))
            o = 512 * (g % 2)
            dst = TAB[:, g, 0:4, :].rearrange("p s m -> p (s m)")
            if g % 2 == 0:
                sch.add("act", lambda e, bk=bk, dst=dst, o=o: e.activation(out=dst, in_=psb[bk][:, o:o + 512], func=AF.Copy), r=(("ps", bk),), w=tk(g))
            else:
                sch.add("dve", lambda e, bk=bk, dst=dst, o=o: e.tensor_copy(dst, psb[bk][:, o:o + 512]), r=(("ps", bk),), w=tk(g))
        if 's5stop1' in dbg:
            barrier()
            return
        sch.add("pool", lambda e: e.memset(ZB[:, :, :, 0:1], 0.0), w=("zcol0",))
        zc = 0
        for g in range(NG):
            for ri in range(2):
                bk = pb()
                for i in range(2):
                    sch.add("pe", lambda e, bk=bk, g=g, ri=ri, i=i: e.matmul(psf[bk][:, :], lhsT=TAB[:, g, 2 * ri + i, :], rhs=UALL[:, g, i::2],
                                                                             start=(i == 0), stop=(i == 1)),
                            r=(("tab", g, 2 * ri + i), "uall"), w=(("ps", bk),))
                for half in range(2):
                    ps_in = psf[bk][0:64, :] if half == 0 else psf[bk][64:128, ::-1]
                    zo = ZB[64 * half:64 * half + 64, ri, g, 1:513]
                    if True:
                        sch.add("act", lambda e, ps_in=ps_in, zo=zo: e.activation(out=zo, in_=ps_in, func=AF.Copy), r=(("ps", bk),), w=(("zbg", g, ri, half),))
                    else:
                        sch.add("dve", lambda e, ps_in=ps_in, zo=zo: e.tensor_copy(zo, ps_in), r=(("ps", bk),), w=(("zbg", g, ri, half),))
                zc += 1
        if 's5stop2' in dbg:
            barrier()
            return
        QT0 = TAB[:, :, 0:2, :].rearrange("p g s (t h) -> p g (s t) h", h=16)
        QT1 = TAB[:, :, 2:4, :].rearrange("p g s (t h) -> p g (s t) h", h=16)
        if 'noqt' not in dbg:
            outer(QT0, QT1, 16, CR, CI, "cr", "ci", 16,
                  lambda q: tuple(("tab", g_, s_) for g_ in range(8 * q, 8 * q + 8) for s_ in (0, 1)),
                  lambda q: tuple(("tab", g_, s_) for g_ in range(8 * q, 8 * q + 8) for s_ in (2, 3)), neg_im=True)
        outer(PBR, PBI, 32, CR, CI, "cr", "ci", 16, "pbr", "pbi", neg_im=True)
        for half in range(2):
            outer(PNM[half][0], PNM[half][1], 48, BBR, BBI, "bbr", "bbi", 8, ("pnm", half, 0), ("pnm", half, 1))
            op_ = slice(64 * (1 - half), 64 * (1 - half) + 64)
            for ri in range(2):
                sch.add("dve", lambda e, half=half, ri=ri, op_=op_: e.memset(PNM[half][ri][op_], 0.0), r=(("pnm", half, ri),), w=(("pnm", half, ri),))
        for g in range(NG if 'notgen' not in dbg else 0):
            bk = pb()
            for (sl, half, dl) in ((0, 0, 0), (1, 1, 0), (2, 0, 1), (3, 1, 1)):
                sch.add("pe", lambda e, bk=bk, sl=sl, half=half, dl=dl, g=g: e.matmul(psf[bk][:, 128 * sl:128 * sl + 128],
                                                                                       lhsT=PNM[half][0][:, g, :, :].rearrange("p k h -> p (k h)"),
                                                                                       rhs=PBR[:, g, 8 * dl:8 * dl + 8, :].rearrange("p t h -> p (t h)"),
                                                                                       start=True, stop=False),
                        r=(("pnm", half, 0), "pbr"), w=(("ps", bk),))
                sch.add("pe", lambda e, bk=bk, sl=sl, half=half, dl=dl, g=g: e.matmul(psf[bk][:, 128 * sl:128 * sl + 128],
                                                                                       lhsT=PNM[half][1][:, g, :, :].rearrange("p k h -> p (k h)"),
                                                                                       rhs=PBI[:, g, 8 * dl:8 * dl + 8, :].rearrange("p t h -> p (t h)"),
                                                                                       start=False, stop=True),
                        r=(("pnm", half, 1), "pbi"), w=(("ps", bk),))
            sch.add("dve", lambda e, bk=bk: e.tensor_tensor(out=TMPA, in0=psf[bk][:, 0:128], in1=MASKF, op=ALU.mult), r=(("ps", bk), "maskf"), w=("tmpa",))
            sch.add("dve", lambda e, bk=bk: e.tensor_tensor(out=TMPB, in0=psf[bk][:, 128:256], in1=MASKB, op=ALU.mult), r=(("ps", bk), "maskb"), w=("tmpb",))
            sch.add("dve", lambda e: e.tensor_tensor(out=TMPA, in0=TMPA, in1=TMPB, op=ALU.add), r=("tmpa", "tmpb"), w=("tmpa",))
            if 'tg_nostt' in dbg:
                sch.add("dve", lambda e, g=g: e.tensor_copy(TAB[:, g, 4, :], TMPA), r=("tmpa",), w=(("tab", g, 4),))
            else:
                sch.add("dve", lambda e, g=g: e.scalar_tensor_tensor(out=TAB[:, g, 4, :], in0=IDENTF, scalar=DSK[:, g:g + 1], in1=TMPA, op0=ALU.mult, op1=ALU.add),
                        r=("tmpa", "dsk", "identf"), w=(("tab", g, 4),))
            if 'tg_noact' not in dbg:
                sch.add("dve", lambda e, bk=bk, g=g: e.tensor_copy(TAB[:, g, 5:7, :].rearrange("p s m -> p (s m)"), psf[bk][:, 256:512]),
                        r=(("ps", bk),), w=(("tab", g, 5), ("tab", g, 6)))
        if 's5stop3' in dbg:
            barrier()
            return
        if 'dbgtab' in dbg and gh == 0:
            d_tab = nc.dram_tensor("d_tab", [128, NG * 7 * 128], BF16, kind="ExternalOutput").ap()
            d_er = nc.dram_tensor("d_er", [128, NG * NE], F32, kind="ExternalOutput").ap()
            d_ei = nc.dram_tensor("d_ei", [128, NG * NE], F32, kind="ExternalOutput").ap()
            d_bbr = nc.dram_tensor("d_bbr", [128, NG * 16], F32, kind="ExternalOutput").ap()
            d_zb = nc.dram_tensor("d_zb", [128, 2 * NG * 513], BF16, kind="ExternalOutput").ap()
            tall = tuple(("tab", g_, s_) for g_ in range(NG) for s_ in range(7))
            dma("sp", d_tab[:, :], TAB.rearrange("p g s m -> p (g s m)"), tall, "d_tab")
            dma("sp", d_er[:, :], ER.rearrange("p g j -> p (g j)"), ("er",), "d_er")
            dma("sp", d_ei[:, :], EI.rearrange("p g j -> p (g j)"), ("ei",), "d_ei")
            dma("sp", d_bbr[:, :], BBR.rearrange("p g j -> p (g j)"), ("bbr",), "d_bbr")
            dma("sp", d_zb[:, :], ZB.rearrange("p r g c -> p (r g c)"), tuple(("zbg", g_, r_, h_) for g_ in range(NG) for r_ in range(2) for h_ in range(2)) + ("zcol0",), "d_zb")
        zbkeys = tuple(("zbg", g_, r_, h_) for g_ in range(NG) for r_ in range(2) for h_ in range(2))
        dead = ("er", "ei", "mag", "ang", "scy", "sci", "scf", "t1", "t2", "pbr", "pbi") + tuple(("pnm", h_, r_) for h_ in range(2) for r_ in range(2))
        dma("sp", CIDX, k_cidx[:, :], (), "cidx")
        for q4 in range(4):
            gq = slice(4 * q4, 4 * q4 + 4)
            sch.add("dve", lambda e, gq=gq: e.tensor_tensor(out=WK[0], in0=PHI[:, gq].unsqueeze(2).broadcast_to([128, 4, 512]),
                                                            in1=CIDX.unsqueeze(1).broadcast_to([128, 4, 512]), op=ALU.mult),
                    r=("phi", "cidx"), w=("wk0",) + (dead if q4 == 0 else ()))
            for (OUT, off, okey) in ((SINB, 0.0, ("sinb", q4)), (COSB, 0.25, ("cosb", q4))):
                sch.add("dve", lambda e, off=off: e.tensor_scalar(WK[1], WK[0], 1.0 / (2 * np.pi), off, op0=ALU.mult, op1=ALU.add), r=("wk0",), w=("wk1",))
                sch.add("dve", lambda e: e.tensor_copy(WKI, WK[1]), r=("wk1",), w=("wk2",))
                sch.add("dve", lambda e: e.tensor_tensor(out=WK[1], in0=WK[1], in1=WKI, op=ALU.subtract), r=("wk1", "wk2"), w=("wk1",))
                sch.add("act", lambda e, OUT=OUT, gq=gq: e.activation(out=OUT[:, gq, :], in_=WK[1], func=AF.Sin, scale=float(2 * np.pi * (1.0 - 2e-6))), r=("wk1",), w=(okey,))

        def rotate(sign, q4, rkeys, wkey):
            gq = slice(4 * q4, 4 * q4 + 4)
            zr = ZB[:, 0, gq, 1:513]
            zi = ZB[:, 1, gq, 1:513]
            cb_ = COSB[:, gq, :]
            sb_ = SINB[:, gq, :]
            rk = rkeys + (("sinb", q4), ("cosb", q4))
            sch.add("dve", lambda e: e.tensor_tensor(out=WK[0], in0=cb_, in1=zr, op=ALU.mult), r=rk, w=("wk0",))
            sch.add("dve", lambda e: e.tensor_tensor(out=WK[1], in0=sb_, in1=zi, op=ALU.mult), r=rk, w=("wk1",))
            sch.add("dve", lambda e: e.tensor_tensor(out=WK[2], in0=WK[0], in1=WK[1], op=(ALU.add if sign < 0 else ALU.subtract)), r=("wk0", "wk1"), w=("wk2",))
            sch.add("dve", lambda e: e.tensor_tensor(out=WK[0], in0=cb_, in1=zi, op=ALU.mult), r=rk + ("wk2",), w=("wk0",))
            sch.add("dve", lambda e: e.tensor_tensor(out=WK[1], in0=sb_, in1=zr, op=ALU.mult), r=rk + ("wk2",), w=("wk1",))
            sch.add("dve", lambda e: e.tensor_tensor(out=zi, in0=WK[0], in1=WK[1], op=(ALU.subtract if sign < 0 else ALU.add)), r=("wk0", "wk1"), w=(wkey,))
            sch.add("act", lambda e: e.activation(out=zr, in_=WK[2], func=AF.Copy), r=("wk2", wkey), w=(wkey,))

        for q4 in range(4):
            rotate(-1, q4, zbkeys, ("zw", q4))
        for q4 in range(4):
            for g in range(4 * q4, 4 * q4 + 4):
                for ri in range(2):
                    zz = ZB[:, ri, g, 1:513]
                    sch.add("dve", lambda e, zz=zz, g=g: e.tensor_tensor_scan(out=zz, data0=RHO[:, g:g + 1].broadcast_to([128, 512]), data1=zz, initial=0.0,
                                                                                  op0=ALU.mult, op1=ALU.add),
                            r=(("zw", q4), "rho"), w=(("zr", q4),))
        for q4 in range(4):
            rotate(+1, q4, (("zr", q4),), ("zc", q4))
        zall = tuple(("zc", q4) for q4 in range(4))
        for ri in range(2):
            src0 = ZB[0:64, ri, :, 0:512].rearrange("p g (a b) -> p g a b", b=8)[:, :, :, 0:4]
            src1 = ZB[64:128, ri, :, 511::-1].rearrange("p g (a b) -> p g a b", b=8)[:, :, :, 0:4]
            d0 = SQ[0:64, ri].rearrange("p g (a b) -> p g a b", b=4)
            d1 = SQ[64:128, ri].rearrange("p g (a b) -> p g a b", b=4)
            sch.add("dve", lambda e, d0=d0, src0=src0: e.tensor_copy(d0, src0), r=zall + ("zcol0",), w=("sq", "er", "ei", "mag", "ang", "scy", "sci", "scf"))
            sch.add("dve", lambda e, d1=d1, src1=src1: e.tensor_copy(d1, src1), r=zall + ("zcol0",), w=("sq",))
        if 's5stop5' in dbg:
            barrier()
            return
        for g in range(NG):
            bk = pb()
            ug = UALL[:, g, :].rearrange("p (a b i) -> p a b i", b=8, i=2)
            po = psf[bk][:, :].rearrange("p (a b i) -> p a b i", b=4, i=2)
            for io in range(2):
                ops_ = ((4, ug[:, :, 0:4, io], "uall"), (5 if io == 1 else 6, ug[:, :, 0:4, 1 - io], "uall"),
                        (0 + io, SQ[:, 0, g, :].rearrange("p (a b) -> p a b", b=4), "sq"), (2 + io, SQ[:, 1, g, :].rearrange("p (a b) -> p a b", b=4), "sq"))
                for n_, (sl, rhs, rk) in enumerate(ops_):
                    sch.add("pe", lambda e, bk=bk, sl=sl, rhs=rhs, g=g, io=io, n_=n_, po=po: e.matmul(po[:, :, :, io], lhsT=TAB[:, g, sl, :], rhs=rhs,
                                                                                                      start=(n_ == 0), stop=(n_ == 3)),
                            r=(("tab", g, sl), rk), w=(("ps", bk),))
            ys = (g % 2)
            sch.add("act", lambda e, bk=bk, ys=ys: e.activation(out=YGS[ys], in_=psf[bk][:, :], func=AF.Gelu_apprx_tanh), r=(("ps", bk),), w=(("ygs", ys),))
            gg = g0 + g
            for t in range(8):
                dma("sp", Yg[16 * gg:16 * gg + 16, :].rearrange("h (T t j) -> h T t j", T=8, t=8)[:, :, t, :],
                    YGS[ys][16 * t:16 * t + 16, :].rearrange("p (T j) -> p T j", T=8), (("ygs", ys),), "Yg", nobar=True)
        barrier()

    if "s5" in phases:
        s5_half(0)
        s5_half(1)

    if "fn" in phases:
        F_ = RG(0)
        FA = F_.alloc([128], BF16)
        FC = F_.alloc([2, 2, 256], BF16)
        FG = F_.alloc([32, 2, 256], BF16)
        UG = [F_.alloc([64, 256], BF16) for _ in range(2)]
        YS = [F_.alloc([64, 256], BF16) for _ in range(2)]
        dma("sp", FA, k_fa[:, :], (), "fa")
        dma("sp", FC.rearrange("p a b c -> p (a b c)"), k_fc[:, :], (), "fc")
        dma("sp", FG.rearrange("p a b c -> p (a b c)"), k_fg[:, :], (), "fg")
        Ufn_v = Ufn.rearrange("(s2 s1) c -> s2 s1 c", s1=64)
        dma("act", UG[0], Ufn_v[:, :, 0:256], ("Ufn",), ("ug", 0))
        for gq in range(6):
            sl = gq % 2
            if gq + 1 < 6:
                dma("act", UG[1 - sl], Ufn_v[:, :, 256 * (gq + 1):256 * (gq + 1) + 256], ("Ufn",), ("ug", 1 - sl))
            for nt in range(32):
                bk = pb()
                sch.add("pe", lambda e, bk=bk, sl=sl, nt=nt: e.matmul(psf[bk][:, :], lhsT=FA, rhs=UG[sl][:, 2 * nt:2 * nt + 2, :], start=True, stop=True),
                        r=("fa", ("ug", sl)), w=(("ps", bk),))
                dst = YS[sl][:, 2 * nt:2 * nt + 2, :].rearrange("p a c -> p (a c)")
                if nt % 2 == 0:
                    sch.add("act", lambda e, bk=bk, dst=dst: e.activation(out=dst, in_=psf[bk][:, :], func=AF.Copy), r=(("ps", bk),), w=(("ys", sl, nt),))
                else:
                    sch.add("dve", lambda e, bk=bk, dst=dst: e.tensor_copy(dst, psf[bk][:, :]), r=(("ps", bk),), w=(("ys", sl, nt),))
            dma("sp", Yd[:, :, 256 * gq:256 * gq + 256], YS[sl], tuple(("ys", sl, nt) for nt in range(32)), "Yd")
        barrier()
        F2 = Region(big, F_.off - 4 * 64 * 256 * 2)
        XB = F2.alloc([2, 2, OWN], BF16)
        YB = [F2.alloc([2, 256], BF16) for _ in range(4)]
        ZST = [F2.alloc([512], BF16) for _ in range(2)]
        Yd_v = Yd.rearrange("(r k) s c -> r k s c", r=2)
        zi = 0
        def yb_load(n):
            gq_, j_ = n // 32, n % 32
            dma("act", YB[n % 4], Yd_v[:, 2 * j_:2 * j_ + 2, :, 256 * gq_:256 * gq_ + 256].rearrange("r b s c -> (b s) r c"), ("Yd",), ("yb", n % 4))

        for n in range(3):
            yb_load(n)
        for gq in range(6):
            for j in range(32):
                n_it = 32 * gq + j
                ysl = n_it % 4
                if n_it + 3 < 192:
                    yb_load(n_it + 3)
                t0 = (2 * j) % 8
                x0 = (2 * j) // 8
                for cb in range(2):
                    bk = pb()
                    for ri in range(2):
                        sch.add("pe", lambda e, bk=bk, ysl=ysl, cb=cb, ri=ri, j=j: e.matmul(psf[bk][:, 0:256], lhsT=YB[ysl][:, ri, 128 * cb:128 * cb + 128],
                                                                                             rhs=FG[:, j, ri, :], start=(ri == 0), stop=(ri == 1)),
                                r=(("yb", ysl), "fg"), w=(("ps", bk),))
                    for ro in range(2):
                        src = psf[bk][:, 128 * ro:128 * ro + 128].rearrange("p (b T k) -> p T b k", b=2, T=8)
                        dst = XB[:, cb, ro, :].rearrange("p (T t k x) -> p T t k x", T=8, t=8, k=8)[:, :, t0:t0 + 2, :, x0]
                        if cb == 0:
                            sch.add("act", lambda e, src=src, dst=dst: e.activation(out=dst, in_=src, func=AF.Copy), r=(("ps", bk),), w=(("x", cb, ro, j),))
                        else:
                            sch.add("dve", lambda e, src=src, dst=dst: e.tensor_copy(dst, src), r=(("ps", bk),), w=(("x", cb, ro, j),))
            for kb in range(2):
                for lt in range(8):
                    bk = pb()
                    n_ = 0
                    for cb in range(2):
                        for ri in range(2):
                            sch.add("pe", lambda e, bk=bk, cb=cb, ri=ri, kb=kb, lt=lt, n_=n_: e.matmul(psf[bk][:, :], lhsT=FC[:, cb, ri, 128 * kb:128 * kb + 128],
                                                                                                       rhs=XB[:, cb, ri, 512 * lt:512 * lt + 512],
                                                                                                       start=(n_ == 0), stop=(n_ == 3)),
                                    r=("fc",) + tuple(("x", cb, ri, j) for j in range(32)), w=(("ps", bk),))
                            n_ += 1
                    zs = zi % 2
                    zi += 1
                    if zs == 0:
                        sch.add("act", lambda e, bk=bk, zs=zs: e.activation(out=ZST[zs], in_=psf[bk][:, :], func=AF.Copy), r=(("ps", bk),), w=(("zst", zs),))
                    else:
                        sch.add("dve", lambda e, bk=bk, zs=zs: e.tensor_copy(ZST[zs], psf[bk][:, :]), r=(("ps", bk),), w=(("zst", zs),))
                    row0 = 256 * gq + 128 * kb
                    dma("sp", Yfn[row0:row0 + 128, 512 * lt:512 * lt + 512], ZST[zs], (("zst", zs),), "Yfn")
        barrier()

    if "p4" in phases:
        Q_ = RG(0)
        HT4 = Q_.alloc([16, 512], BF16)
        SCR = Q_.alloc([44, 512], BF16)
        X1 = Q_.alloc([4, 2048], F32)
        NSL = 5
        WS = [Q_.alloc([4096], BF16) for _ in range(NSL)]
        XN4 = [Q_.alloc([2048], BF16) for _ in range(4)]
        GM = Q_.alloc([2048], F32)
        GF = Q_.alloc([2048], F32)
        NF = Q_.alloc([2048], F32)
        TG = [Q_.alloc([512], F32) for _ in range(4)]
        dma("sp", GM, modrow[0:1, :].partition_broadcast(128), ("modrow",), "gm")
        dma("sp", GF, modrow[1:2, :].partition_broadcast(128), ("modrow",), "gf")
        dma("sp", NF, nfin_row[0:1, :].partition_broadcast(128), (), "nf")
        ring = [0]

        def slab(W, wkey, k0, kn, c0, ncols):
            sl = ring[0] % NSL
            ring[0] += 1
            v = WS[sl][:, 0:kn * ncols].rearrange("p (k c) -> p k c", c=ncols)
            dma("sp", v, W.rearrange("(k p) n -> p k n", p=128)[:, k0:k0 + kn, c0:c0 + ncols], (wkey,), ("ws", sl))
            return v, ("ws", sl)

        xs_own = xs.rearrange("(T j1 hf j0 t) d -> T hf t j1 j0 d", T=8, j1=8, hf=2, j0=8, t=8)
        tgc = [0]
        ec4 = [0]
        def p4_first(T_, rs_):
            for r in rs_:
                for tb in range(2):
                    dma("pool", X1[64 * tb:64 * tb + 64, r, :], xs_own[T_, 0, 2 * r + tb], (), ("x1", r))
                rms_block(X1[:, r, :], ("x1", r), XN4[r], ("xn4", r), 8 + r)

        def p4_final(T_, rs_):
            for r in rs_:
                c1 = slice(8 + r, 9 + r)
                sch.add("act", lambda e, r=r, c1=c1: e.activation(out=JUNK, in_=X1[:, r, :], func=AF.Square, accum_out=SS[:, c1]), r=(("x1", r),), w=("junk", ("ss", 8 + r)))
                sch.add("act", lambda e, c1=c1: e.activation(out=SD[:, c1], in_=SS[:, c1], func=AF.Sqrt, scale=1.0 / D, bias=EPSB[:, 0:1]), r=(("ss", 8 + r), "epsb"), w=(("sd", 8 + r),))
                sch.add("dve", lambda e, c1=c1: e.reciprocal(RS[:, c1], SD[:, c1]), r=(("sd", 8 + r),), w=(("rs", 8 + r),))
                sch.add("dve", lambda e, r=r, c1=c1: e.scalar_tensor_tensor(out=X1[:, r, :], in0=X1[:, r, :], scalar=RS[:, c1], in1=NF, op0=ALU.mult, op1=ALU.mult),
                        r=(("x1", r), ("rs", 8 + r), "nf"), w=(("x1", r),))
                dma("pool", yo[512 * T_ + 128 * r:512 * T_ + 128 * r + 128, :], X1[:, r, :], (("x1", r),), "yo")

        for T in range(8):
            if T == 0:
                p4_first(0, (0, 1, 2, 3))
            for r in range(4):
                transpose_mod(XN4[r], ("xn4", r), HT4, ("ht4", r), r, A1, B1, "a1", "modt", ec4)
            hk = tuple(("ht4", r) for r in range(4))
            dma_multi("pool", SCR[:, 16:20, :], Yg.rearrange("(k p) n -> p k n", p=128)[:, :, 512 * T:512 * T + 512], ("Yg",),
                      [("scr", c_) for c_ in range(16, 20)], "scr_yg")
            dma_multi("pool", SCR[:, 24:36, :], Yfn.rearrange("(k p) n -> p k n", p=128)[:, :, 512 * T:512 * T + 512], ("Yfn",),
                      [("scr", c_) for c_ in range(24, 36)], "scr_yfn")
            wv, wk = slab(wb_glu, "wb_glu", 0, 4, 0, 1024)
            for cbk in range(4):
                bv = pb()
                bg = pb()
                for kc in range(4):
                    sch.add("pe", lambda e, bv=bv, kc=kc, cbk=cbk, wv=wv: e.matmul(psf[bv][:, :], lhsT=wv[:, kc, 128 * cbk:128 * cbk + 128], rhs=SCR[:, 16 + kc, :],
                                                                                   start=(kc == 0), stop=(kc == 3)), r=(wk, ("scr", 16 + kc)), w=(("ps", bv),))
                for kc in range(4):
                    sch.add("pe", lambda e, bg=bg, kc=kc, cbk=cbk, wv=wv: e.matmul(psf[bg][:, :], lhsT=wv[:, kc, 512 + 128 * cbk:640 + 128 * cbk], rhs=SCR[:, 16 + kc, :],
                                                                                   start=(kc == 0), stop=(kc == 3)), r=(wk, ("scr", 16 + kc)), w=(("ps", bg),))
                tg = tgc[0] % 4
                tgc[0] += 1
                sch.add("act", lambda e, bg=bg, tg=tg: e.activation(out=TG[tg], in_=psf[bg][:, :], func=AF.Sigmoid), r=(("ps", bg),), w=(("tg", tg),))
                sch.add("dve", lambda e, bv=bv, tg=tg, cbk=cbk: e.tensor_tensor(out=SCR[:, 20 + cbk, :], in0=psf[bv][:, :], in1=TG[tg], op=ALU.mult),
                        r=(("ps", bv), ("tg", tg)), w=(("scr", 20 + cbk),))
            ysk = tuple(("scr", 20 + c_) for c_ in range(4))
            for hq in range(8):
                ga, gak = slab(wb_gate, "wb_gate", 0, 16, 256 * hq, 256)
                ws_, wsk = slab(wb_bs, "wb_bs", 0, 4, 256 * hq, 256)
                gb, gbk = slab(wb_gate, "wb_gate", 0, 16, 2048 + 256 * hq, 256)
                wf_, wfk = slab(wb_bf, "wb_bf", 0, 12, 256 * hq, 256)
                for f2 in range(2):
                    fb = 2 * hq + f2
                    cs = slice(128 * f2, 128 * f2 + 128)
                    b1, b2, b3, b4 = pb(), pb(), pb(), pb()
                    for kc in range(16):
                        sch.add("pe", lambda e, b1=b1, kc=kc, cs=cs, ga=ga: e.matmul(psf[b1][:, :], lhsT=ga[:, kc, cs], rhs=HT4[:, kc, :], start=(kc == 0), stop=(kc == 15)),
                                r=(gak,) + hk, w=(("ps", b1),))
                    for kc in range(4):
                        sch.add("pe", lambda e, b2=b2, kc=kc, cs=cs, ws_=ws_: e.matmul(psf[b2][:, :], lhsT=ws_[:, kc, cs], rhs=SCR[:, 20 + kc, :], start=(kc == 0), stop=(kc == 3)),
                                r=(wsk,) + ysk, w=(("ps", b2),))
                    for kc in range(16):
                        sch.add("pe", lambda e, b3=b3, kc=kc, cs=cs, gb=gb: e.matmul(psf[b3][:, :], lhsT=gb[:, kc, cs], rhs=HT4[:, kc, :], start=(kc == 0), stop=(kc == 15)),
                                r=(gbk,) + hk, w=(("ps", b3),))
                    for kc in range(12):
                        sch.add("pe", lambda e, b4=b4, kc=kc, cs=cs, wf_=wf_: e.matmul(psf[b4][:, :], lhsT=wf_[:, kc, cs], rhs=SCR[:, 24 + kc, :], start=(kc == 0), stop=(kc == 11)),
                                r=(wfk, ("scr", 24 + kc)), w=(("ps", b4),))
                    ta = 2 * (fb % 2)
                    tb = ta + 1
                    sch.add("act", lambda e, b1=b1, ta=ta: e.activation(out=TG[ta], in_=psf[b1][:, :], func=AF.Sigmoid), r=(("ps", b1),), w=(("tg", ta),))
                    sch.add("act", lambda e, b3=b3, tb=tb: e.activation(out=TG[tb], in_=psf[b3][:, :], func=AF.Sigmoid), r=(("ps", b3),), w=(("tg", tb),))
                    sch.add("dve", lambda e, b2=b2, ta=ta: e.tensor_tensor(out=TG[ta], in0=psf[b2][:, :], in1=TG[ta], op=ALU.mult), r=(("ps", b2), ("tg", ta)), w=(("tg", ta),))
                    sch.add("dve", lambda e, b4=b4, tb=tb: e.tensor_tensor(out=TG[tb], in0=psf[b4][:, :], in1=TG[tb], op=ALU.mult), r=(("ps", b4), ("tg", tb)), w=(("tg", tb),))
                    sch.add("dve", lambda e, fb=fb, ta=ta, tb=tb: e.tensor_tensor(out=SCR[:, fb, :], in0=TG[ta], in1=TG[tb], op=ALU.add), r=(("tg", ta), ("tg", tb)), w=(("scr", fb),))
            mgk = tuple(("scr", fb) for fb in range(16))
            for nb in range(4):
                bks = [pb() for _ in range(4)]
                for kg in range(2):
                    wo, wok = slab(wb_out, "wb_out", 8 * kg, 8, 512 * nb, 512)
                    for r in range(4):
                        for k_ in range(8):
                            kc = 8 * kg + k_
                            sch.add("pe", lambda e, bk=bks[r], kc=kc, k_=k_, r=r, wo=wo: e.matmul(psf[bk][:, :], lhsT=SCR[:, kc, 128 * r:128 * r + 128], rhs=wo[:, k_, :],
                                                                                                   start=(kc == 0), stop=(kc == 15)), r=(wok,) + mgk, w=(("ps", bks[r]),))
                cs = slice(512 * nb, 512 * nb + 512)
                for r in range(4):
                    tq = r % 4
                    sch.add("dve", lambda e, bk=bks[r], cs=cs, tq=tq: e.tensor_tensor(out=TG[tq], in0=psf[bk][:, :], in1=GM[:, cs], op=ALU.mult), r=(("ps", bks[r]), "gm"), w=(("tg", tq),))
                    sch.add("dve", lambda e, r=r, cs=cs, tq=tq: e.tensor_tensor(out=X1[:, r, cs], in0=X1[:, r, cs], in1=TG[tq], op=ALU.add), r=(("tg", tq), ("x1", r)), w=(("x1", r),))
            if "dbg_x1" in dbg:
                for r in range(4):
                    dma("sp", dbg_x1[512 * T + 128 * r:512 * T + 128 * r + 128, :], X1[:, r, :], (("x1", r),), "dbg_x1")
            if "dbg_mg" in dbg:
                dma("sp", dbg_mg[:, :, 512 * T:512 * T + 512], SCR[:, 0:24, :], mgk + ysk + tuple(("scr", c_) for c_ in range(16, 20)), "dbg_mg")
            for r in range(4):
                rms_block(X1[:, r, :], ("x1", r), XN4[r], ("xn4", r), 12 + r)
            for r in range(4):
                transpose_mod(XN4[r], ("xn4", r), HT4, ("ht4", r), r, A2, B2, "a2", "modt", ec4)
            for hq in range(22):
                wa, wak = slab(wb_f1, "wb_f1", 0, 16, 256 * hq, 256)
                wb_, wbk = slab(wb_f1, "wb_f1", 0, 16, DFF + 256 * hq, 256)
                for f2 in range(2):
                    cb = 2 * hq + f2
                    cs = slice(128 * f2, 128 * f2 + 128)
                    ba, bb = pb(), pb()
                    for kc in range(16):
                        sch.add("pe", lambda e, ba=ba, kc=kc, cs=cs, wa=wa: e.matmul(psf[ba][:, :], lhsT=wa[:, kc, cs], rhs=HT4[:, kc, :], start=(kc == 0), stop=(kc == 15)),
                                r=(wak,) + hk, w=(("ps", ba),))
                    for kc in range(16):
                        sch.add("pe", lambda e, bb=bb, kc=kc, cs=cs, wb_=wb_: e.matmul(psf[bb][:, :], lhsT=wb_[:, kc, cs], rhs=HT4[:, kc, :], start=(kc == 0), stop=(kc == 15)),
                                r=(wbk,) + hk, w=(("ps", bb),))
                    tg = tgc[0] % 4
                    tgc[0] += 1
                    sch.add("act", lambda e, ba=ba, tg=tg: e.activation(out=TG[tg], in_=psf[ba][:, :], func=AF.Silu), r=(("ps", ba),), w=(("tg", tg),))
                    sch.add("dve", lambda e, bb=bb, tg=tg, cb=cb: e.tensor_tensor(out=SCR[:, cb, :], in0=psf[bb][:, :], in1=TG[tg], op=ALU.mult),
                            r=(("ps", bb), ("tg", tg)), w=(("scr", cb),))
            ack = tuple(("scr", cb) for cb in range(44))
            for rp in range(2):
                rs_ = (2 * rp, 2 * rp + 1)
                for nb in range(4):
                    bks = {r: pb() for r in rs_}
                    for kg in range(6):
                        kn = 8 if kg < 5 else 4
                        w2, w2k = slab(wb_f2, "wb_f2", 8 * kg, kn, 512 * nb, 512)
                        for r in rs_:
                            for k_ in range(kn):
                                kc = 8 * kg + k_
                                sch.add("pe", lambda e, bk=bks[r], kc=kc, k_=k_, r=r, w2=w2: e.matmul(psf[bk][:, :], lhsT=SCR[:, kc, 128 * r:128 * r + 128], rhs=w2[:, k_, :],
                                                                                                       start=(kc == 0), stop=(kc == 43)), r=(w2k,) + ack, w=(("ps", bks[r]),))
                    cs = slice(512 * nb, 512 * nb + 512)
                    for r in rs_:
                        tq = r % 4
                        sch.add("dve", lambda e, bk=bks[r], cs=cs, tq=tq: e.tensor_tensor(out=TG[tq], in0=psf[bk][:, :], in1=GF[:, cs], op=ALU.mult), r=(("ps", bks[r]), "gf"), w=(("tg", tq),))
                        sch.add("dve", lambda e, r=r, cs=cs, tq=tq: e.tensor_tensor(out=X1[:, r, cs], in0=X1[:, r, cs], in1=TG[tq], op=ALU.add), r=(("tg", tq), ("x1", r)), w=(("x1", r),))
                p4_final(T, rs_)
                if T + 1 < 8:
                    p4_first(T + 1, rs_)
    return nc, sch, stack, locals()


def kernel(**inputs):
    inp = {k: np.asarray(v) for k, v in inputs.items()}
    nc, sch, stack, _ = build_nc()
    sch.add("sp", None, r=("yo",))
    sch.emit(nc, stack)
    stack.close()
    in_maps = []
    for core in range(8):
        b, e = core // 2, core % 2
        in_maps.append(_prep_core(inp, b, e, _const_tables(e)))
    res = run_bass_kernel_spmd(nc, in_maps, core_ids=list(range(8)))
    out = np.zeros((4, S, D), np.float32)
    pos = np.arange(OWN)
    T = pos // 512
    q = pos % 512
    t = q // 64
    jj = q % 64
    l = 512 * T + 8 * jj + t
    kp = 128 * (l // 64) + (l % 64)
    for core in range(8):
        b, e = core // 2, core % 2
        act = kp if e == 0 else (S - 1 - kp)
        out[b, act] = res.results[core]["yo"]
    return out
```

```python
import contextlib
import numpy as np
import ml_dtypes
import concourse.bass as bass
import concourse.mybir as mybir
from concourse.bass_utils import run_bass_kernel_spmd

F32 = mybir.dt.float32
BF16 = mybir.dt.bfloat16
AF = mybir.ActivationFunctionType
ALU = mybir.AluOpType

D = 2048
S = 8192
OWN = 4096
DFF = 5632
EPS = 1e-6
NE = 58


class Op:
    __slots__ = ("eng", "fn", "r", "w", "dma", "deps", "signal", "ev", "semkey")

    def __init__(self, eng, fn, r, w, dma, semkey):
        self.eng = eng
        self.fn = fn
        self.r = r
        self.w = w
        self.dma = dma
        self.deps = ()
        self.signal = False
        self.ev = None
        self.semkey = semkey


class Sched:
    ENGS = ("pe", "act", "dve", "pool", "sp")

    def __init__(self):
        self.ops = []

    def add(self, eng, fn, r=(), w=(), dma=False, semkey=None, nobar=False):
        r = tuple(r)
        if not nobar:
            r = r + ("PHASE",)
        self.ops.append(Op(eng, fn, r, tuple(w), dma, semkey))

    def barrier(self, fn):
        self.ops.append(Op("pool", fn, (), ("PHASE",), False, None))

    def analyze(self):
        ops = self.ops
        last_w = {}
        rd_eng = {}
        rd_dma = {}
        for i, op in enumerate(ops):
            deps = {}
            for k in op.r:
                j = last_w.get(k)
                if j is not None:
                    deps[j] = True
            for k in op.w:
                j = last_w.get(k)
                if j is not None:
                    deps.setdefault(j, False)
                for j in rd_eng.get(k, {}).values():
                    deps.setdefault(j, False)
                for j in rd_dma.get(k, ()):
                    deps.setdefault(j, False)
            keep = []
            for j, raw in deps.items():
                if j == i:
                    continue
                oj = ops[j]
                if (not oj.dma) and (not op.dma) and oj.eng == op.eng and not raw:
                    continue
                keep.append(j)
                oj.signal = True
            op.deps = keep
            for k in op.r:
                if op.dma:
                    rd_dma.setdefault(k, []).append(i)
                else:
                    rd_eng.setdefault(k, {})[op.eng] = i
            for k in op.w:
                last_w[k] = i
                rd_eng[k] = {}
                rd_dma[k] = []

    def emit(self, nc, stack):
        self.analyze()
        ops = self.ops
        sems = {}

        def getsem(name):
            if name not in sems:
                sems[name] = stack.enter_context(nc.semaphore("s_" + str(len(sems))))
            return sems[name]

        cnt = {}
        for op in ops:
            if op.dma:
                name = ("dma", op.semkey if op.semkey is not None else op.w[0])
                cnt[name] = cnt.get(name, 0) + 16
                op.ev = (name, cnt[name])
            elif op.signal:
                name = ("eng", op.eng)
                cnt[name] = cnt.get(name, 0) + 1
                op.ev = (name, cnt[name])
        for name in cnt:
            getsem(name)
        self.nsems = len(sems)
        for sh in sems.values():
            nc.gpsimd.sem_clear(sh)
        nc.all_engine_barrier()
        block = stack.enter_context(nc.Block())

        def make_body(eng):
            def body(e):
                waited = {}
                for op in ops:
                    if op.eng != eng:
                        continue
                    need = {}
                    for j in op.deps:
                        s, v = ops[j].ev
                        if need.get(s, 0) < v:
                            need[s] = v
                    for s, v in need.items():
                        if waited.get(s, 0) < v:
                            e.wait_ge(sems[s], v)
                            waited[s] = v
                    if op.fn is not None:
                        ins = op.fn(e)
                        if op.dma:
                            ins.then_inc(sems[op.ev[0]], 16)
                        elif op.signal:
                            ins.then_inc(sems[op.ev[0]], 1)
            return body

        block.sync(make_body("sp"))
        block.scalar(make_body("act"))
        block.vector(make_body("dve"))
        block.gpsimd(make_body("pool"))
        block.tensor(make_body("pe"))


def _bf(a):
    return np.asarray(a, dtype=np.float32).astype(ml_dtypes.bfloat16)


def _const_tables(e):
    c = {}
    c["ident"] = _bf(np.eye(128))
    expo = np.zeros((128, NE), np.float32)
    for d in range(2):
        rows = slice(64 * d, 64 * d + 64)
        for tau in range(16):
            expo[rows, tau] = (15 - tau) if d == 0 else tau
            expo[rows, 16 + tau] = (tau + 1) if d == 0 else (16 - tau)
        for dl in range(2):
            for t in range(8):
                expo[rows, 32 + 8 * dl + t] = (8 * dl + t) if d == 0 else (8 * dl + 7 - t)
        for k in range(8):
            expo[rows, 48 + k] = (-k) if d == 0 else (k - 7)
        expo[rows, 56] = 16
        expo[rows, 57] = 1
    c["expo"] = expo
    kk = np.arange(128) // 16
    mf = (kk[None, :] >= kk[:, None]).astype(np.float32)
    mb = (kk[:, None] >= kk[None, :]).astype(np.float32)
    c["cidx"] = np.tile(np.arange(512, dtype=np.float32)[None, :], (128, 1))
    c["maskf"] = mf
    c["maskb"] = mb
    s2l = np.arange(128)
    k2l = np.arange(64)
    s2 = s2l if e == 0 else 127 - s2l
    k2 = k2l if e == 0 else 127 - k2l
    ph = 2 * np.pi * (s2[:, None].astype(np.float64) * k2[None, :]) / 128.0
    sa = 2.0 ** -5
    c["fa"] = _bf(np.concatenate([sa * np.cos(ph), -sa * np.sin(ph)], axis=1))
    s1l = np.arange(64)
    k1l = np.arange(64)
    s1 = s1l if e == 0 else 63 - s1l
    k1 = k1l if e == 0 else 63 - k1l
    gb = np.zeros((128, 32, 2, 2, 2, 64), np.float64)
    for j in range(32):
        for b in range(2):
            k2a = k2[2 * j + b]
            phi = 2 * np.pi * (s1[:, None] * k1[None, :] / 64.0 + s1[:, None] * float(k2a) / 8192.0)
            gr = sa * np.cos(phi)
            gi = -sa * np.sin(phi)
            rows = slice(64 * b, 64 * b + 64)
            gb[rows, j, 0, 0, b, :] = gr
            gb[rows, j, 0, 1, b, :] = gi
            gb[rows, j, 1, 0, b, :] = -gi
            gb[rows, j, 1, 1, b, :] = gr
    c["fg"] = _bf(gb.reshape(128, 32 * 2 * 256))
    ch = np.arange(256)
    kc = np.arange(256)
    phc = 2 * np.pi * (ch[:, None].astype(np.float64) * kc[None, :]) / 256.0
    sc = 2.0 ** -0.5
    ct = np.zeros((128, 2, 2, 256), np.float64)
    for cb in range(2):
        ct[:, cb, 0, :] = sc * np.cos(phc[128 * cb:128 * cb + 128])
        ct[:, cb, 1, :] = sc * np.sin(phc[128 * cb:128 * cb + 128])
    c["fc"] = _bf(ct.reshape(128, 1024))
    return c


def _prep_core(inp, b, e, consts):
    m = {}
    x = inp["x"][b]
    m["xs"] = np.ascontiguousarray(x if e == 0 else x[::-1])
    m["cvec"] = np.ascontiguousarray(inp["c"][b].reshape(16, 128).T)
    m["b_adaT"] = np.ascontiguousarray(inp["b_ada"][0].reshape(96, 128).T)
    m["b_ada_row"] = np.ascontiguousarray(inp["b_ada"][0].reshape(1, 12288))
    m["nmixT"] = np.ascontiguousarray(inp["norm_mix"][0].reshape(16, 128).T)
    m["nffnT"] = np.ascontiguousarray(inp["norm_ffn"][0].reshape(16, 128).T)
    m["nfin_row"] = np.ascontiguousarray(inp["norm_final"].reshape(1, D))
    dsel = [0, 1] if e == 0 else [1, 0]

    def dn_g(a):
        return np.ascontiguousarray(a[dsel].transpose(0, 2, 1).reshape(128, 32))

    m["lamre"] = dn_g(inp["s5_lambda_re"][0])
    m["lamim"] = dn_g(inp["s5_lambda_im"][0])
    ls = inp["s5_log_step"][0][dsel]
    m["lstep"] = np.ascontiguousarray(np.repeat(ls[:, None, :], 64, axis=1).reshape(128, 32))
    m["bre"] = np.ascontiguousarray(inp["s5_b_re"][0][dsel].transpose(0, 2, 1, 3).reshape(128, 512))
    m["bim"] = np.ascontiguousarray(inp["s5_b_im"][0][dsel].transpose(0, 2, 1, 3).reshape(128, 512))
    m["cre"] = np.ascontiguousarray(inp["s5_c_re"][0][dsel].transpose(0, 3, 1, 2).reshape(128, 512))
    m["cim"] = np.ascontiguousarray(inp["s5_c_im"][0][dsel].transpose(0, 3, 1, 2).reshape(128, 512))
    dsk = inp["s5_d"][0].reshape(32, 16)
    m["dskip"] = np.ascontiguousarray(np.tile(dsk.T[None, :, :], (8, 1, 1)).reshape(128, 32))
    for k in ("w_ada", "w_in", "w_s5_glu", "w_branch_s5", "w_branch_fnet", "w_out", "w_ffn_in", "w_ffn_out"):
        m[k] = inp[k][0]
    for k, v in consts.items():
        m["k_" + k] = v
    return m


SB_WORDS = 48640
PBASE = 4096


class Region:
    def __init__(self, big, base):
        self.big = big
        self.off = base

    def alloc(self, shape, dt):
        es = 4 if dt == F32 else 2
        n = 1
        for s in shape:
            n *= s
        nbytes = (n * es + 31) // 32 * 32
        off = self.off
        self.off += nbytes
        assert self.off <= SB_WORDS * 4, ("sbuf overflow", self.off)
        ap = self.big[:, off // 4:(off + nbytes) // 4]
        if dt != F32:
            ap = ap.bitcast(dt)
        ap = ap[:, 0:n]
        if len(shape) == 2:
            ap = ap.rearrange("p (a b) -> p a b", a=shape[0], b=shape[1])
        elif len(shape) == 3:
            ap = ap.rearrange("p (a b c) -> p a b c", a=shape[0], b=shape[1], c=shape[2])
        elif len(shape) == 4:
            ap = ap.rearrange("p (a b c d) -> p a b c d", a=shape[0], b=shape[1], c=shape[2], d=shape[3])
        return ap


def build_nc(phases=("p0", "p1", "s5", "fn", "p4"), dbg=()):
    nc = bass.Bass("TRN2", target_bir_lowering=False)
    stack = contextlib.ExitStack()
    sch = Sched()

    def din(name, shape, dt=F32):
        return nc.dram_tensor(name, list(shape), dt, kind="ExternalInput").ap()

    def dscr(name, shape, dt, out=False):
        if out or name in dbg:
            return nc.dram_tensor(name, list(shape), dt, kind="ExternalOutput").ap()
        return nc.dram_tensor(name, list(shape), dt).ap()

    xs = din("xs", [S, D])
    cvec = din("cvec", [128, 16])
    b_adaT = din("b_adaT", [128, 96])
    b_ada_row = din("b_ada_row", [1, 12288])
    nmixT = din("nmixT", [128, 16])
    nffnT = din("nffnT", [128, 16])
    nfin_row = din("nfin_row", [1, D])
    lamre = din("lamre", [128, 32])
    lamim = din("lamim", [128, 32])
    lstep = din("lstep", [128, 32])
    bre = din("bre", [128, 512])
    bim = din("bim", [128, 512])
    cre = din("cre", [128, 512])
    cim = din("cim", [128, 512])
    dskip = din("dskip", [128, 32])
    w_ada = din("w_ada", [D, 12288])
    w_in = din("w_in", [D, 6144])
    w_glu = din("w_s5_glu", [512, 1024])
    w_bs = din("w_branch_s5", [512, D])
    w_bf = din("w_branch_fnet", [1536, D])
    w_out = din("w_out", [D, D])
    w_f1 = din("w_ffn_in", [D, 2 * DFF])
    w_f2 = din("w_ffn_out", [DFF, D])
    k_ident = din("k_ident", [128, 128], BF16)
    k_expo = din("k_expo", [128, NE])
    k_maskf = din("k_maskf", [128, 128])
    k_cidx = din("k_cidx", [128, 512])
    k_maskb = din("k_maskb", [128, 128])
    k_fa = din("k_fa", [128, 128], BF16)
    k_fg = din("k_fg", [128, 32 * 512], BF16)
    k_fc = din("k_fc", [128, 1024], BF16)

    yo = dscr("yo", [OWN, D], F32, out=True)
    wb_gate = dscr("wb_gate", [D, 4096], BF16)
    wb_glu = dscr("wb_glu", [512, 1024], BF16)
    wb_bs = dscr("wb_bs", [512, D], BF16)
    wb_bf = dscr("wb_bf", [1536, D], BF16)
    wb_out = dscr("wb_out", [D, D], BF16)
    wb_f1 = dscr("wb_f1", [D, 2 * DFF], BF16)
    wb_f2 = dscr("wb_f2", [DFF, D], BF16)
    Ufn = dscr("Ufn", [S, 1536], BF16)
    Us = dscr("Us", [32, 128, 1024], BF16)
    Yd = dscr("Yd", [128, 64, 1536], BF16)
    Yfn = dscr("Yfn", [1536, OWN], BF16)
    Yg = dscr("Yg", [512, OWN], BF16)
    modrow = dscr("modrow", [2, D], F32)
    dbg_modT = dscr("dbg_modT", [128, 96], F32) if "dbg_modT" in dbg else None
    dbg_x1 = dscr("dbg_x1", [OWN, D], F32) if "dbg_x1" in dbg else None
    dbg_mg = dscr("dbg_mg", [128, 24, OWN], BF16) if "dbg_mg" in dbg else None

    big = stack.enter_context(nc.sbuf_tensor("big", [128, SB_WORDS], F32))
    psf = [stack.enter_context(nc.psum_tensor("ps%d" % i, [128, 512], F32)) for i in range(8)]
    psb = [p[:, :].bitcast(BF16) for p in psf]
    pstate = {"i": 0}

    def pb():
        i = pstate["i"]
        pstate["i"] = (i + 1) % 8
        return i

    PR = Region(big, 0)
    IDENT = PR.alloc([128], BF16)
    ONES = PR.alloc([128], F32)
    MODT = PR.alloc([96], F32)
    A1 = PR.alloc([16], F32)
    A2 = PR.alloc([16], F32)
    CV = PR.alloc([16], F32)
    CACT = PR.alloc([16], F32)
    NMT = PR.alloc([16], F32)
    NFT = PR.alloc([16], F32)
    BAT = PR.alloc([96], F32)
    SS = PR.alloc([16], F32)
    SD = PR.alloc([16], F32)
    RS = PR.alloc([16], F32)
    DUMMY = PR.alloc([8], F32)
    assert PR.off <= PBASE
    B1 = MODT[:, 0:16]
    B2 = MODT[:, 48:64]

    def dma(eng, out, in_, r, wkey, semkey=None, nobar=False):
        sch.add(eng, lambda e: e.dma_start(out=out, in_=in_), r=r, w=(wkey,), dma=True, semkey=semkey, nobar=nobar)

    def dma_multi(eng, out, in_, r, wkeys, semkey):
        sch.add(eng, lambda e: e.dma_start(out=out, in_=in_), r=r, w=tuple(wkeys), dma=True, semkey=semkey)

    def barrier():
        sch.barrier(lambda e: e.memset(DUMMY[:, 0:1], 0.0))

    def cast_weight(dst, src, rows, key, c0=0, ncols=None):
        ncols = ncols if ncols is not None else dst.shape[1]
        for r0 in range(0, rows, 128):
            dma("pool", dst[r0:r0 + 128, :], src[r0:r0 + 128, c0:c0 + ncols], (), key, nobar=True)

    dma("sp", IDENT, k_ident[:, :], (), "ident")
    dma("sp", CV, cvec[:, :], (), "cv")
    dma("sp", BAT, b_adaT[:, :], (), "bat")
    dma("sp", NMT, nmixT[:, :], (), "nmt")
    dma("sp", NFT, nffnT[:, :], (), "nft")
    sch.add("pool", lambda e: e.memset(ONES, 1.0), w=("ones",))
    sch.add("act", lambda e: e.activation(out=CACT, in_=CV, func=AF.Silu), r=("cv",), w=("cact",))
    R1 = Region(big, PBASE)
    JUNK = R1.alloc([2048], BF16)
    WIN = R1.alloc([16, 2048], BF16)
    if "p1" in phases:
        w_in_v = w_in.rearrange("(kc p) n -> p kc n", p=128)
        for q in range(4):
            dma("pool", WIN[:, 4 * q:4 * q + 4, :], w_in_v[:, 4 * q:4 * q + 4, 0:2048], (), ("win", q))
    if "p0" in phases:
        R0 = Region(big, PBASE + 80 * 1024)
        ACC = R0.alloc([4096], F32)
        SLAB = [R0.alloc([4096], F32) for _ in range(3)]
        BROW = R0.alloc([2048], F32)
        ROWST = R0.alloc([2048], F32)
        si = 0
        for part in range(3):
            for kc in range(16):
                sl = si % 3
                si += 1
                dma("sp", SLAB[sl], w_ada[128 * kc:128 * kc + 128, 4096 * part:4096 * part + 4096], (), ("wa", sl))
                if kc == 0:
                    sch.add("dve", lambda e, sl=sl, kc=kc: e.tensor_scalar(ACC, SLAB[sl], CACT[:, kc:kc + 1], None, op0=ALU.mult),
                            r=(("wa", sl), "cact"), w=("acc",))
                else:
                    sch.add("dve", lambda e, sl=sl, kc=kc: e.scalar_tensor_tensor(out=ACC, in0=SLAB[sl], scalar=CACT[:, kc:kc + 1], in1=ACC,
                                                                               op0=ALU.mult, op1=ALU.add),
                            r=(("wa", sl), "cact", "acc"), w=("acc",))
            if 'nope' in dbg:
                continue
            bk = pb()
            for j in range(32):
                sch.add("pe", lambda e, j=j, bk=bk: e.matmul(psf[bk][:, j:j + 1], lhsT=ACC[:, 128 * j:128 * j + 128], rhs=ONES[:, 0:1],
                                                             start=True, stop=True),
                        r=("acc", "ones"), w=(("ps", bk),))
            sch.add("dve", lambda e, bk=bk, part=part: e.tensor_tensor(out=MODT[:, 32 * part:32 * part + 32], in0=psf[bk][:, 0:32],
                                                                        in1=BAT[:, 32 * part:32 * part + 32], op=ALU.add),
                    r=(("ps", bk), "bat"), w=("modt",))
            if part >= 1 and 'norow' not in dbg:
                coff = 0 if part == 1 else 2048
                gcol = 4096 if part == 1 else 10240
                dma("sp", BROW, b_ada_row[0:1, gcol:gcol + 2048].partition_broadcast(128), (), "brow")
                for nb in range(4):
                    bk = pb()
                    sch.add("pe", lambda e, bk=bk, nb=nb, coff=coff: e.matmul(psf[bk][:, :], lhsT=ONES, rhs=ACC[:, coff + 512 * nb:coff + 512 * nb + 512],
                                                                             start=True, stop=True),
                            r=("acc", "ones"), w=(("ps", bk),))
                    sch.add("dve", lambda e, bk=bk, nb=nb: e.tensor_tensor(out=ROWST[:, 512 * nb:512 * nb + 512], in0=psf[bk][:, :],
                                                                           in1=BROW[:, 512 * nb:512 * nb + 512], op=ALU.add),
                            r=(("ps", bk), "brow"), w=("rowst",))
                dma("sp", modrow[part - 1:part, :], ROWST[0:1, :], ("rowst",), "modrow")
        sch.add("dve", lambda e: e.scalar_tensor_tensor(out=A1, in0=MODT[:, 16:32], scalar=1.0, in1=NMT, op0=ALU.add, op1=ALU.mult),
                r=("modt", "nmt"), w=("a1",))
        sch.add("dve", lambda e: e.scalar_tensor_tensor(out=A2, in0=MODT[:, 64:80], scalar=1.0, in1=NFT, op0=ALU.add, op1=ALU.mult),
                r=("modt", "nft"), w=("a2",))
        if dbg_modT is not None:
            dma("sp", dbg_modT[:, :], MODT, ("modt",), "dbg_modT")
        barrier()

    def rms_block(xt_ap, xkey, xn_ap, xnkey, col):
        c1 = slice(col, col + 1)
        sch.add("act", lambda e: e.activation(out=JUNK, in_=xt_ap, func=AF.Square, accum_out=SS[:, c1]),
                r=(xkey,), w=("junk", ("ss", col)))
        sch.add("act", lambda e: e.activation(out=SD[:, c1], in_=SS[:, c1], func=AF.Sqrt, scale=1.0 / D, bias=EPSB[:, 0:1]),
                r=(("ss", col), "epsb"), w=(("sd", col),))
        sch.add("dve", lambda e: e.reciprocal(RS[:, c1], SD[:, c1]), r=(("sd", col),), w=(("rs", col),))
        sch.add("dve", lambda e: e.tensor_scalar(xn_ap, xt_ap, RS[:, c1], None, op0=ALU.mult),
                r=(xkey, ("rs", col)), w=(xnkey,))

    def transpose_mod(xn_ap, xnkey, ht_ap, htkey, r, Acol, Bcol, akey, bkey, cnt):
        for cg in range(2):
            bk = pb()
            for c8 in range(8):
                ch = 8 * cg + c8
                sch.add("pe", lambda e, bk=bk, c8=c8, ch=ch: e.transpose(psb[bk][:, 128 * c8:128 * c8 + 128], xn_ap[:, 128 * ch:128 * ch + 128], IDENT),
                        r=(xnkey, "ident"), w=(("ps", bk),))
            for c8 in range(8):
                ch = 8 * cg + c8
                src = psb[bk][:, 128 * c8:128 * c8 + 128]
                dst = ht_ap[:, ch, 128 * r:128 * r + 128]
                if cg == 0:
                    sch.add("act", lambda e, src=src, dst=dst, ch=ch: e.activation(out=dst, in_=src, func=AF.Identity, scale=Acol[:, ch:ch + 1], bias=Bcol[:, ch:ch + 1]),
                            r=(("ps", bk), akey, bkey), w=(htkey,))
                else:
                    sch.add("dve", lambda e, src=src, dst=dst, ch=ch: e.tensor_scalar(dst, src, Acol[:, ch:ch + 1], Bcol[:, ch:ch + 1], op0=ALU.mult, op1=ALU.add),
                            r=(("ps", bk), akey, bkey), w=(htkey,))
                cnt[0] += 1

    EPSB = PR.alloc([8], F32)
    sch.add("pool", lambda e: e.memset(EPSB, EPS), w=("epsb",))
    if "p1" in phases:
        XT = [R1.alloc([2048], F32) for _ in range(4)]
        XN = [R1.alloc([2048], BF16) for _ in range(4)]
        HT = [R1.alloc([16, 512], BF16) for _ in range(2)]
        UFN = [R1.alloc([1536], BF16) for _ in range(2)]
        US5 = R1.alloc([4, 8, 256], BF16)
        winkeys = tuple(("win", q) for q in range(4))
        if "p4" in phases:
            cast_weight(wb_gate, w_in, D, "wb_gate", c0=2048, ncols=4096)
            cast_weight(wb_glu, w_glu, 512, "wb_glu")
            cast_weight(wb_bs, w_bs, 512, "wb_bs")
            cast_weight(wb_bf, w_bf, 1536, "wb_bf")
            cast_weight(wb_out, w_out, D, "wb_out")
            cast_weight(wb_f1, w_f1, D, "wb_f1")
            cast_weight(wb_f2, w_f2, DFF, "wb_f2")
        ecnt = [0]
        def p1_rms(tt):
            for r in range(4):
                dma("sp", XT[r], xs[512 * tt + 128 * r:512 * tt + 128 * r + 128, :], (), ("xt", r))
                rms_block(XT[r], ("xt", r), XN[r], ("xn", r), (4 * tt + r) % 8)

        p1_rms(0)

        def p1_fnet(tt, hs, r):
            us = (4 * tt + r) % 2
            bks = [pb() for _ in range(3)]
            for kc in range(16):
                for nb in range(3):
                    sch.add("pe", lambda e, kc=kc, nb=nb, r=r, hs=hs, bk=bks[nb]: e.matmul(psf[bk][:, :], lhsT=HT[hs][:, kc, 128 * r:128 * r + 128],
                                                                                            rhs=WIN[:, kc, 512 + 512 * nb:1024 + 512 * nb],
                                                                                            start=(kc == 0), stop=(kc == 15)),
                            r=(("ht", hs, r),) + winkeys, w=(("ps", bks[nb]),))
            for nb in range(3):
                if nb == 1:
                    sch.add("act", lambda e, nb=nb, us=us, bk=bks[nb]: e.activation(out=UFN[us][:, 512 * nb:512 * nb + 512], in_=psf[bk][:, :], func=AF.Copy),
                            r=(("ps", bks[nb]),), w=(("ufn", us),))
                else:
                    sch.add("dve", lambda e, nb=nb, us=us, bk=bks[nb]: e.tensor_copy(UFN[us][:, 512 * nb:512 * nb + 512], psf[bk][:, :]),
                            r=(("ps", bks[nb]),), w=(("ufn", us),))
            row0 = 512 * tt + 128 * r
            dma("act", Ufn[row0:row0 + 128, :], UFN[us], (("ufn", us),), "Ufn")

        for tt in range(S // 512):
            hs = tt % 2
            for r in range(4):
                transpose_mod(XN[r], ("xn", r), HT[hs], ("ht", hs, r), r, A1, B1, "a1", "modt", ecnt)
                if r >= 1:
                    p1_fnet(tt, hs, r - 1)
            if tt + 1 < S // 512:
                p1_rms(tt + 1)
            p1_fnet(tt, hs, 3)
            hkeys = tuple(("ht", hs, r) for r in range(4))
            for blk in range(4):
                bk = pb()
                for kc in range(16):
                    sch.add("pe", lambda e, bk=bk, kc=kc, blk=blk, hs=hs: e.matmul(psf[bk][:, :], lhsT=WIN[:, kc, 128 * blk:128 * blk + 128], rhs=HT[hs][:, kc, :],
                                                                                   start=(kc == 0), stop=(kc == 15)),
                            r=hkeys + winkeys, w=(("ps", bk),))
                j0 = 64 * (tt % 4)
                sch.add("act", lambda e, bk=bk, blk=blk, j0=j0: e.activation(out=US5[:, blk, :, j0:j0 + 64], in_=psf[bk][:, :].rearrange("p (j k) -> p k j", k=8),
                                                                             func=AF.Copy),
                        r=(("ps", bk),), w=(("us5", blk),))
            if tt % 4 == 3:
                rnd = tt // 4
                for blk in range(4):
                    for gl in range(8):
                        g = 8 * blk + gl
                        dma("sp", Us[g].rearrange("(k h) j -> h k j", h=16)[:, :, 256 * rnd:256 * rnd + 256],
                            US5[16 * gl:16 * gl + 16, blk, :, :], (("us5", blk),), "Us")
        barrier()

    I32 = mybir.dt.int32
    KB = 1024
    base0 = PBASE + 4 * KB

    def RG(off_kb):
        return Region(big, base0 + off_kb * KB)

    def s5_half(gh):
        GE = "dve"
        NG = 16
        g0 = NG * gh
        SP_ = RG(0)
        LR, LI, LS, DT, LRD, TH, DSK = [SP_.alloc([NG], F32) for _ in range(7)]
        t32 = [SP_.alloc([NG], F32) for _ in range(7)]
        BR, BI, CR, CI, BBR, BBI = [SP_.alloc([NG, 16], F32) for _ in range(6)]
        EXPO = SP_.alloc([NE], F32)
        AR2, AI2, S0, S1, TA, TB, TC = [SP_.alloc([2, NG], F32) for _ in range(7)]
        SST = [S0, S1]
        assert SP_.off <= base0 + 8 * KB, SP_.off
        E_ = RG(8)
        MAG, ANG, SCY, SCIf, SCF, ER, EI = [E_.alloc([NG, NE], F32) for _ in range(7)]
        SCI = SCIf.bitcast(I32)
        assert E_.off <= base0 + 34 * KB
        SQ = RG(8).alloc([2, NG, 256], BF16)
        T_ = RG(34)
        T1 = T_.alloc([8, 16, 16], F32)
        T2 = T_.alloc([8, 16, 16], F32)
        P_ = RG(50)
        PBR = P_.alloc([NG, 16, 16], BF16)
        PBI = P_.alloc([NG, 16, 16], BF16)
        PNM = [[P_.alloc([NG, 8, 16], BF16) for _ in range(2)] for _ in range(2)]
        assert P_.off <= base0 + 82 * KB
        TAB = RG(82).alloc([NG, 7, 128], BF16)
        ZB = RG(110).alloc([2, NG, 513], BF16)
        UALL = RG(143).alloc([NG, 1024], BF16)
        Y_ = RG(175)
        YGS = [Y_.alloc([512], BF16) for _ in range(2)]
        MASKF = Y_.alloc([128], F32)
        MASKB = Y_.alloc([128], F32)
        TMPA = Y_.alloc([128], F32)
        TMPB = Y_.alloc([128], F32)
        IDENTF = Y_.alloc([128], F32)
        RHO = Y_.alloc([NG], F32)
        PHI = Y_.alloc([NG], F32)
        PHY = Y_.alloc([NG], F32)
        PHYI = Y_.alloc([NG], F32).bitcast(I32)
        CIDX = RG(8).alloc([512], F32) if False else None
        COSB = RG(24).alloc([NG, 512], BF16)
        SINB = RG(40).alloc([NG, 512], BF16)
        WK = [RG(56 + 8 * i_).alloc([4, 512], F32) for i_ in range(3)]
        WKI = WK[2].bitcast(I32)
        CIDX = Y_.alloc([512], F32)
        assert Y_.off <= SB_WORDS * 4
        sch.add("dve", lambda e: e.tensor_copy(IDENTF, IDENT), r=("ident",), w=("identf",))
        gs = slice(g0, g0 + NG)
        for (dst, src, k) in ((LR, lamre, "lr"), (LI, lamim, "li"), (LS, lstep, "ls"), (DSK, dskip, "dsk")):
            dma("sp", dst, src[:, gs], (), k)
        dma("sp", EXPO, k_expo[:, :], (), "expo")
        dma("sp", MASKF, k_maskf[:, :], (), "maskf")
        dma("sp", MASKB, k_maskb[:, :], (), "maskb")
        hs_ = slice(16 * g0, 16 * g0 + 16 * NG)
        dma("sp", BR.rearrange("p g h -> p (g h)"), bre[:, hs_], (), "br")
        dma("sp", BI.rearrange("p g h -> p (g h)"), bim[:, hs_], (), "bi")
        dma("sp", CR.rearrange("p g h -> p (g h)"), cre[:, hs_], (), "cr")
        dma("sp", CI.rearrange("p g h -> p (g h)"), cim[:, hs_], (), "ci")
        dma("sp", UALL, Us[gs].rearrange("g p j -> p g j"), ("Us",), "uall")
        sch.add("act", lambda e: e.activation(out=DT, in_=LS, func=AF.Exp), r=("ls",), w=("dt",))
        sch.add(GE, lambda e: e.tensor_tensor(out=LRD, in0=LR, in1=DT, op=ALU.mult), r=("lr", "dt"), w=("lrd",))
        sch.add(GE, lambda e: e.tensor_tensor(out=TH, in0=LI, in1=DT, op=ALU.mult), r=("li", "dt"), w=("th",))

        def bc_g(x, n):
            return x.unsqueeze(2).broadcast_to([128, NG, n])

        def bc_e(x, n):
            return x.unsqueeze(1).broadcast_to([128, NG, n])

        fl = lambda a: a.rearrange("p g j -> p (g j)")
        sch.add(GE, lambda e: e.tensor_tensor(out=MAG, in0=bc_g(LRD, NE), in1=bc_e(EXPO, NE), op=ALU.mult), r=("lrd", "expo"), w=("mag",))
        sch.add("act", lambda e: e.activation(out=MAG, in_=MAG, func=AF.Exp), r=("mag",), w=("mag",))
        sch.add(GE, lambda e: e.tensor_tensor(out=ANG, in0=bc_g(TH, NE), in1=bc_e(EXPO, NE), op=ALU.mult), r=("th", "expo"), w=("ang",))
        for (OUT, off, okey) in ((EI, 0.0, "ei"), (ER, 0.25, "er")):
            sch.add(GE, lambda e, off=off: e.tensor_scalar(SCY, ANG, 1.0 / (2 * np.pi), off, op0=ALU.mult, op1=ALU.add), r=("ang",), w=("scy",))
            sch.add(GE, lambda e: e.tensor_copy(SCI, SCY), r=("scy",), w=("sci",))
            sch.add(GE, lambda e: e.tensor_copy(SCF, SCI), r=("sci",), w=("scf",))
            sch.add(GE, lambda e: e.tensor_tensor(out=SCY, in0=SCY, in1=SCF, op=ALU.subtract), r=("scy", "scf"), w=("scy",))
            sch.add("act", lambda e, OUT=OUT: e.activation(out=OUT, in_=SCY, func=AF.Sin, scale=float(2 * np.pi * (1.0 - 2e-6))), r=("scy",), w=(okey,))
            sch.add(GE, lambda e, OUT=OUT: e.tensor_tensor(out=OUT, in0=OUT, in1=MAG, op=ALU.mult), r=(okey, "mag"), w=(okey,))
        numr, den, cr_, ci_, x1, x2, rden = t32
        a1r = ER[:, :, 57]
        a1i = EI[:, :, 57]
        sch.add(GE, lambda e: e.tensor_scalar(numr, a1r, -1.0, None, op0=ALU.add), r=("er",), w=("numr",))
        sch.add(GE, lambda e: e.tensor_tensor(out=x1, in0=LR, in1=LR, op=ALU.mult), r=("lr",), w=("x1",))
        sch.add(GE, lambda e: e.tensor_tensor(out=x2, in0=LI, in1=LI, op=ALU.mult), r=("li",), w=("x2",))
        sch.add(GE, lambda e: e.tensor_tensor(out=den, in0=x1, in1=x2, op=ALU.add), r=("x1", "x2"), w=("den",))
        sch.add(GE, lambda e: e.reciprocal(rden, den), r=("den",), w=("rden",))
        sch.add(GE, lambda e: e.tensor_tensor(out=x1, in0=numr, in1=LR, op=ALU.mult), r=("numr", "lr", "den"), w=("x1",))
        sch.add(GE, lambda e: e.tensor_tensor(out=x2, in0=a1i, in1=LI, op=ALU.mult), r=("ei", "li", "den"), w=("x2",))
        sch.add(GE, lambda e: e.tensor_tensor(out=cr_, in0=x1, in1=x2, op=ALU.add), r=("x1", "x2"), w=("cr_",))
        sch.add(GE, lambda e: e.tensor_tensor(out=cr_, in0=cr_, in1=rden, op=ALU.mult), r=("cr_", "rden"), w=("cr_",))
        sch.add(GE, lambda e: e.tensor_tensor(out=x1, in0=a1i, in1=LR, op=ALU.mult), r=("ei", "lr", "cr_"), w=("x1",))
        sch.add(GE, lambda e: e.tensor_tensor(out=x2, in0=numr, in1=LI, op=ALU.mult), r=("numr", "li", "cr_"), w=("x2",))
        sch.add(GE, lambda e: e.tensor_tensor(out=ci_, in0=x1, in1=x2, op=ALU.subtract), r=("x1", "x2"), w=("ci_",))
        sch.add(GE, lambda e: e.tensor_tensor(out=ci_, in0=ci_, in1=rden, op=ALU.mult), r=("ci_", "rden"), w=("ci_",))
        B1s = T1.rearrange("p a b c -> p (a b c)")[:, 0:256].rearrange("p (g h) -> p g h", h=16)
        B2s = T2.rearrange("p a b c -> p (a b c)")[:, 0:256].rearrange("p (g h) -> p g h", h=16)
        sch.add(GE, lambda e: e.tensor_tensor(out=B1s, in0=BR, in1=bc_g(cr_, 16), op=ALU.mult), r=("br", "cr_"), w=("t1",))
        sch.add(GE, lambda e: e.tensor_tensor(out=B2s, in0=BI, in1=bc_g(ci_, 16), op=ALU.mult), r=("bi", "ci_"), w=("t2",))
        sch.add(GE, lambda e: e.tensor_tensor(out=BBR, in0=B1s, in1=B2s, op=ALU.subtract), r=("t1", "t2"), w=("bbr",))
        sch.add(GE, lambda e: e.tensor_tensor(out=B1s, in0=BI, in1=bc_g(cr_, 16), op=ALU.mult), r=("bi", "cr_", "bbr"), w=("t1",))
        sch.add(GE, lambda e: e.tensor_tensor(out=B2s, in0=BR, in1=bc_g(ci_, 16), op=ALU.mult), r=("br", "ci_", "bbr"), w=("t2",))
        sch.add(GE, lambda e: e.tensor_tensor(out=BBI, in0=B1s, in1=B2s, op=ALU.add), r=("t1", "t2"), w=("bbi",))
        sch.add(GE, lambda e: e.tensor_copy(AR2[:, 0, :], ER[:, :, 56]), r=("er",), w=("ar2",))
        sch.add(GE, lambda e: e.tensor_copy(AR2[:, 1, :], ER[:, :, 56]), r=("er",), w=("ar2",))
        sch.add(GE, lambda e: e.tensor_scalar(AI2[:, 0, :], EI[:, :, 56], -1.0, None, op0=ALU.mult), r=("ei",), w=("ai2",))
        sch.add(GE, lambda e: e.tensor_copy(AI2[:, 1, :], EI[:, :, 56]), r=("ei",), w=("ai2",))
        sch.add(GE, lambda e: e.tensor_copy(RHO, MAG[:, :, 56]), r=("mag",), w=("rho",))
        sch.add(GE, lambda e: e.tensor_scalar(PHY, TH, 16.0 / (2 * np.pi), None, op0=ALU.mult), r=("th",), w=("phy",))
        sch.add(GE, lambda e: e.tensor_copy(PHYI, PHY), r=("phy",), w=("phyi",))
        sch.add(GE, lambda e: e.tensor_tensor(out=PHY, in0=PHY, in1=PHYI, op=ALU.subtract), r=("phy", "phyi"), w=("phy",))
        sch.add(GE, lambda e: e.tensor_scalar(PHI, PHY, float(2 * np.pi), None, op0=ALU.mult), r=("phy",), w=("phi",))

        def outer(OUTR, OUTI, e0, XR, XI, xkr, xki, nt, okr_, oki_, neg_im=False):
            for q in range(2):
                gq = slice(8 * q, 8 * q + 8)
                okr = okr_(q) if callable(okr_) else (okr_,)
                oki = oki_(q) if callable(oki_) else (oki_,)
                er = ER[:, gq, e0:e0 + nt].unsqueeze(3).broadcast_to([128, 8, nt, 16])
                ei = EI[:, gq, e0:e0 + nt].unsqueeze(3).broadcast_to([128, 8, nt, 16])
                xr = XR[:, gq, :].unsqueeze(2).broadcast_to([128, 8, nt, 16])
                xi = XI[:, gq, :].unsqueeze(2).broadcast_to([128, 8, nt, 16])
                t1 = T1[:, :, 0:nt, :]
                t2 = T2[:, :, 0:nt, :]
                outr = OUTR[:, gq]
                outi = OUTI[:, gq]
                sch.add(GE, lambda e, t1=t1, er=er, xr=xr: e.tensor_tensor(out=t1, in0=er, in1=xr, op=ALU.mult), r=("er", xkr), w=("t1",))
                sch.add(GE, lambda e, t2=t2, ei=ei, xi=xi: e.tensor_tensor(out=t2, in0=ei, in1=xi, op=ALU.mult), r=("ei", xki), w=("t2",))
                sch.add(GE, lambda e, t1=t1, t2=t2, outr=outr: e.tensor_tensor(out=outr, in0=t1, in1=t2, op=ALU.subtract), r=("t1", "t2"), w=okr)
                sch.add(GE, lambda e, t1=t1, er=er, xi=xi: e.tensor_tensor(out=t1, in0=er, in1=xi, op=ALU.mult), r=("er", xki) + okr, w=("t1",))
                sch.add(GE, lambda e, t2=t2, ei=ei, xr=xr: e.tensor_tensor(out=t2, in0=ei, in1=xr, op=ALU.mult), r=("ei", xkr) + okr, w=("t2",))
                if neg_im:
                    sch.add(GE, lambda e, t1=t1, t2=t2, outi=outi: e.scalar_tensor_tensor(out=outi, in0=t1, scalar=-1.0, in1=t2, op0=ALU.mult, op1=ALU.subtract),
                            r=("t1", "t2"), w=oki)
                else:
                    sch.add(GE, lambda e, t1=t1, t2=t2, outi=outi: e.tensor_tensor(out=outi, in0=t1, in1=t2, op=ALU.add), r=("t1", "t2"), w=oki)

        outer(PBR, PBI, 0, BBR, BBI, "bbr", "bbi", 16, "pbr", "pbi")
        tk = lambda g: tuple(("tab", g, s_) for s_ in range(4))
        for g in range(NG):
            if g % 2 == 0:
                bk = pb()
            for ri in range(2):
                src = PBR if ri == 0 else PBI
                for i in range(2):
                    sl = 4 * (g % 2) + 2 * ri + i
                    sch.add("pe", lambda e, bk=bk, sl=sl, src=src, g=g, i=i: e.transpose(psb[bk][:, 128 * sl:128 * sl + 128],
                                                                                         src[:, g, 8 * i:8 * i + 8, :].rearrange("p k h -> p (k h)"), IDENT),
                            r=("pbr", "pbi", "ident"), w=(("ps", bk),))
            o = 512 * (g % 2)
            dst = TAB[:, g, 0:4, :].rearrange("p s m -> p (s m)")
            if g % 2 == 0:
                sch.add("act", lambda e, bk=bk, dst=dst, o=o: e.activation(out=dst, in_=psb[bk][:, o:o + 512], func=AF.Copy), r=(("ps", bk),), w=tk(g))
            else:
                sch.add("dve", lambda e, bk=bk, dst=dst, o=o: e.tensor_copy(dst, psb[bk][:, o:o + 512]), r=(("ps", bk),), w=tk(g))
        if 's5stop1' in dbg:
            barrier()
            return
        sch.add("pool", lambda e: e.memset(ZB[:, :, :, 0:1], 0.0), w=("zcol0",))
        zc = 0
        for g in range(NG):
            for ri in range(2):
                bk = pb()
                for i in range(2):
                    sch.add("pe", lambda e, bk=bk, g=g, ri=ri, i=i: e.matmul(psf[bk][:, :], lhsT=TAB[:, g, 2 * ri + i, :], rhs=UALL[:, g, i::2],
                                                                             start=(i == 0), stop=(i == 1)),
                            r=(("tab", g, 2 * ri + i), "uall"), w=(("ps", bk),))
                for half in range(2):
                    ps_in = psf[bk][0:64, :] if half == 0 else psf[bk][64:128, ::-1]
                    zo = ZB[64 * half:64 * half + 64, ri, g, 1:513]
                    if True:
                        sch.add("act", lambda e, ps_in=ps_in, zo=zo: e.activation(out=zo, in_=ps_in, func=AF.Copy), r=(("ps", bk),), w=(("zbg", g, ri, half),))
                    else:
                        sch.add("dve", lambda e, ps_in=ps_in, zo=zo: e.tensor_copy(zo, ps_in), r=(("ps", bk),), w=(("zbg", g, ri, half),))
                zc += 1
        if 's5stop2' in dbg:
            barrier()
            return
        QT0 = TAB[:, :, 0:2, :].rearrange("p g s (t h) -> p g (s t) h", h=16)
        QT1 = TAB[:, :, 2:4, :].rearrange("p g s (t h) -> p g (s t) h", h=16)
        if 'noqt' not in dbg:
            outer(QT0, QT1, 16, CR, CI, "cr", "ci", 16,
                  lambda q: tuple(("tab", g_, s_) for g_ in range(8 * q, 8 * q + 8) for s_ in (0, 1)),
                  lambda q: tuple(("tab", g_, s_) for g_ in range(8 * q, 8 * q + 8) for s_ in (2, 3)), neg_im=True)
        outer(PBR, PBI, 32, CR, CI, "cr", "ci", 16, "pbr", "pbi", neg_im=True)
        for half in range(2):
            outer(PNM[half][0], PNM[half][1], 48, BBR, BBI, "bbr", "bbi", 8, ("pnm", half, 0), ("pnm", half, 1))
            op_ = slice(64 * (1 - half), 64 * (1 - half) + 64)
            for ri in range(2):
                sch.add("dve", lambda e, half=half, ri=ri, op_=op_: e.memset(PNM[half][ri][op_], 0.0), r=(("pnm", half, ri),), w=(("pnm", half, ri),))
        for g in range(NG if 'notgen' not in dbg else 0):
            bk = pb()
            for (sl, half, dl) in ((0, 0, 0), (1, 1, 0), (2, 0, 1), (3, 1, 1)):
                sch.add("pe", lambda e, bk=bk, sl=sl, half=half, dl=dl, g=g: e.matmul(psf[bk][:, 128 * sl:128 * sl + 128],
                                                                                       lhsT=PNM[half][0][:, g, :, :].rearrange("p k h -> p (k h)"),
                                                                                       rhs=PBR[:, g, 8 * dl:8 * dl + 8, :].rearrange("p t h -> p (t h)"),
                                                                                       start=True, stop=False),
                        r=(("pnm", half, 0), "pbr"), w=(("ps", bk),))
                sch.add("pe", lambda e, bk=bk, sl=sl, half=half, dl=dl, g=g: e.matmul(psf[bk][:, 128 * sl:128 * sl + 128],
                                                                                       lhsT=PNM[half][1][:, g, :, :].rearrange("p k h -> p (k h)"),
                                                                                       rhs=PBI[:, g, 8 * dl:8 * dl + 8, :].rearrange("p t h -> p (t h)"),
                                                                                       start=False, stop=True),
                        r=(("pnm", half, 1), "pbi"), w=(("ps", bk),))
            sch.add("dve", lambda e, bk=bk: e.tensor_tensor(out=TMPA, in0=psf[bk][:, 0:128], in1=MASKF, op=ALU.mult), r=(("ps", bk), "maskf"), w=("tmpa",))
            sch.add("dve", lambda e, bk=bk: e.tensor_tensor(out=TMPB, in0=psf[bk][:, 128:256], in1=MASKB, op=ALU.mult), r=(("ps", bk), "maskb"), w=("tmpb",))
            sch.add("dve", lambda e: e.tensor_tensor(out=TMPA, in0=TMPA, in1=TMPB, op=ALU.add), r=("tmpa", "tmpb"), w=("tmpa",))
            if 'tg_nostt' in dbg:
                sch.add("dve", lambda e, g=g: e.tensor_copy(TAB[:, g, 4, :], TMPA), r=("tmpa",), w=(("tab", g, 4),))
            else:
                sch.add("dve", lambda e, g=g: e.scalar_tensor_tensor(out=TAB[:, g, 4, :], in0=IDENTF, scalar=DSK[:, g:g + 1], in1=TMPA, op0=ALU.mult, op1=ALU.add),
                        r=("tmpa", "dsk", "identf"), w=(("tab", g, 4),))
            if 'tg_noact' not in dbg:
                sch.add("dve", lambda e, bk=bk, g=g: e.tensor_copy(TAB[:, g, 5:7, :].rearrange("p s m -> p (s m)"), psf[bk][:, 256:512]),
                        r=(("ps", bk),), w=(("tab", g, 5), ("tab", g, 6)))
        if 's5stop3' in dbg:
            barrier()
            return
        if 'dbgtab' in dbg and gh == 0:
            d_tab = nc.dram_tensor("d_tab", [128, NG * 7 * 128], BF16, kind="ExternalOutput").ap()
            d_er = nc.dram_tensor("d_er", [128, NG * NE], F32, kind="ExternalOutput").ap()
            d_ei = nc.dram_tensor("d_ei", [128, NG * NE], F32, kind="ExternalOutput").ap()
            d_bbr = nc.dram_tensor("d_bbr", [128, NG * 16], F32, kind="ExternalOutput").ap()
            d_zb = nc.dram_tensor("d_zb", [128, 2 * NG * 513], BF16, kind="ExternalOutput").ap()
            tall = tuple(("tab", g_, s_) for g_ in range(NG) for s_ in range(7))
            dma("sp", d_tab[:, :], TAB.rearrange("p g s m -> p (g s m)"), tall, "d_tab")
            dma("sp", d_er[:, :], ER.rearrange("p g j -> p (g j)"), ("er",), "d_er")
            dma("sp", d_ei[:, :], EI.rearrange("p g j -> p (g j)"), ("ei",), "d_ei")
            dma("sp", d_bbr[:, :], BBR.rearrange("p g j -> p (g j)"), ("bbr",), "d_bbr")
            dma("sp", d_zb[:, :], ZB.rearrange("p r g c -> p (r g c)"), tuple(("zbg", g_, r_, h_) for g_ in range(NG) for r_ in range(2) for h_ in range(2)) + ("zcol0",), "d_zb")
        zbkeys = tuple(("zbg", g_, r_, h_) for g_ in range(NG) for r_ in range(2) for h_ in range(2))
        dead = ("er", "ei", "mag", "ang", "scy", "sci", "scf", "t1", "t2", "pbr", "pbi") + tuple(("pnm", h_, r_) for h_ in range(2) for r_ in range(2))
        dma("sp", CIDX, k_cidx[:, :], (), "cidx")
        for q4 in range(4):
            gq = slice(4 * q4, 4 * q4 + 4)
            sch.add("dve", lambda e, gq=gq: e.tensor_tensor(out=WK[0], in0=PHI[:, gq].unsqueeze(2).broadcast_to([128, 4, 512]),
                                                            in1=CIDX.unsqueeze(1).broadcast_to([128, 4, 512]), op=ALU.mult),
                    r=("phi", "cidx"), w=("wk0",) + (dead if q4 == 0 else ()))
            for (OUT, off, okey) in ((SINB, 0.0, ("sinb", q4)), (COSB, 0.25, ("cosb", q4))):
                sch.add("dve", lambda e, off=off: e.tensor_scalar(WK[1], WK[0], 1.0 / (2 * np.pi), off, op0=ALU.mult, op1=ALU.add), r=("wk0",), w=("wk1",))
                sch.add("dve", lambda e: e.tensor_copy(WKI, WK[1]), r=("wk1",), w=("wk2",))
                sch.add("dve", lambda e: e.tensor_tensor(out=WK[1], in0=WK[1], in1=WKI, op=ALU.subtract), r=("wk1", "wk2"), w=("wk1",))
                sch.add("act", lambda e, OUT=OUT, gq=gq: e.activation(out=OUT[:, gq, :], in_=WK[1], func=AF.Sin, scale=float(2 * np.pi * (1.0 - 2e-6))), r=("wk1",), w=(okey,))

        def rotate(sign, q4, rkeys, wkey):
            gq = slice(4 * q4, 4 * q4 + 4)
            zr = ZB[:, 0, gq, 1:513]
            zi = ZB[:, 1, gq, 1:513]
            cb_ = COSB[:, gq, :]
            sb_ = SINB[:, gq, :]
            rk = rkeys + (("sinb", q4), ("cosb", q4))
            sch.add("dve", lambda e: e.tensor_tensor(out=WK[0], in0=cb_, in1=zr, op=ALU.mult), r=rk, w=("wk0",))
            sch.add("dve", lambda e: e.tensor_tensor(out=WK[1], in0=sb_, in1=zi, op=ALU.mult), r=rk, w=("wk1",))
            sch.add("dve", lambda e: e.tensor_tensor(out=WK[2], in0=WK[0], in1=WK[1], op=(ALU.add if sign < 0 else ALU.subtract)), r=("wk0", "wk1"), w=("wk2",))
            sch.add("dve", lambda e: e.tensor_tensor(out=WK[0], in0=cb_, in1=zi, op=ALU.mult), r=rk + ("wk2",), w=("wk0",))
            sch.add("dve", lambda e: e.tensor_tensor(out=WK[1], in0=sb_, in1=zr, op=ALU.mult), r=rk + ("wk2",), w=("wk1",))
            sch.add("dve", lambda e: e.tensor_tensor(out=zi, in0=WK[0], in1=WK[1], op=(ALU.subtract if sign < 0 else ALU.add)), r=("wk0", "wk1"), w=(wkey,))
            sch.add("act", lambda e: e.activation(out=zr, in_=WK[2], func=AF.Copy), r=("wk2", wkey), w=(wkey,))

        for q4 in range(4):
            rotate(-1, q4, zbkeys, ("zw", q4))
        for q4 in range(4):
            for g in range(4 * q4, 4 * q4 + 4):
                for ri in range(2):
                    zz = ZB[:, ri, g, 1:513]
                    sch.add("dve", lambda e, zz=zz, g=g: e.tensor_tensor_scan(out=zz, data0=RHO[:, g:g + 1].broadcast_to([128, 512]), data1=zz, initial=0.0,
                                                                                  op0=ALU.mult, op1=ALU.add),
                            r=(("zw", q4), "rho"), w=(("zr", q4),))
        for q4 in range(4):
            rotate(+1, q4, (("zr", q4),), ("zc", q4))
        zall = tuple(("zc", q4) for q4 in range(4))
        for ri in range(2):
            src0 = ZB[0:64, ri, :, 0:512].rearrange("p g (a b) -> p g a b", b=8)[:, :, :, 0:4]
            src1 = ZB[64:128, ri, :, 511::-1].rearrange("p g (a b) -> p g a b", b=8)[:, :, :, 0:4]
            d0 = SQ[0:64, ri].rearrange("p g (a b) -> p g a b", b=4)
            d1 = SQ[64:128, ri].rearrange("p g (a b) -> p g a b", b=4)
            sch.add("dve", lambda e, d0=d0, src0=src0: e.tensor_copy(d0, src0), r=zall + ("zcol0",), w=("sq", "er", "ei", "mag", "ang", "scy", "sci", "scf"))
            sch.add("dve", lambda e, d1=d1, src1=src1: e.tensor_copy(d1, src1), r=zall + ("zcol0",), w=("sq",))
        if 's5stop5' in dbg:
            barrier()
            return
        for g in range(NG):
            bk = pb()
            ug = UALL[:, g, :].rearrange("p (a b i) -> p a b i", b=8, i=2)
            po = psf[bk][:, :].rearrange("p (a b i) -> p a b i", b=4, i=2)
            for io in range(2):
                ops_ = ((4, ug[:, :, 0:4, io], "uall"), (5 if io == 1 else 6, ug[:, :, 0:4, 1 - io], "uall"),
                        (0 + io, SQ[:, 0, g, :].rearrange("p (a b) -> p a b", b=4), "sq"), (2 + io, SQ[:, 1, g, :].rearrange("p (a b) -> p a b", b=4), "sq"))
                for n_, (sl, rhs, rk) in enumerate(ops_):
                    sch.add("pe", lambda e, bk=bk, sl=sl, rhs=rhs, g=g, io=io, n_=n_, po=po: e.matmul(po[:, :, :, io], lhsT=TAB[:, g, sl, :], rhs=rhs,
                                                                                                      start=(n_ == 0), stop=(n_ == 3)),
                            r=(("tab", g, sl), rk), w=(("ps", bk),))
            ys = (g % 2)
            sch.add("act", lambda e, bk=bk, ys=ys: e.activation(out=YGS[ys], in_=psf[bk][:, :], func=AF.Gelu_apprx_tanh), r=(("ps", bk),), w=(("ygs", ys),))
            gg = g0 + g
            for t in range(8):
                dma("sp", Yg[16 * gg:16 * gg + 16, :].rearrange("h (T t j) -> h T t j", T=8, t=8)[:, :, t, :],
                    YGS[ys][16 * t:16 * t + 16, :].rearrange("p (T j) -> p T j", T=8), (("ygs", ys),), "Yg")
        barrier()

    if "s5" in phases:
        s5_half(0)
        s5_half(1)

    if "fn" in phases:
        F_ = RG(0)
        FA = F_.alloc([128], BF16)
        FC = F_.alloc([2, 2, 256], BF16)
        FG = F_.alloc([32, 2, 256], BF16)
        UG = [F_.alloc([64, 256], BF16) for _ in range(2)]
        YS = [F_.alloc([64, 256], BF16) for _ in range(2)]
        dma("sp", FA, k_fa[:, :], (), "fa")
        dma("sp", FC.rearrange("p a b c -> p (a b c)"), k_fc[:, :], (), "fc")
        dma("sp", FG.rearrange("p a b c -> p (a b c)"), k_fg[:, :], (), "fg")
        Ufn_v = Ufn.rearrange("(s2 s1) c -> s2 s1 c", s1=64)
        dma("act", UG[0], Ufn_v[:, :, 0:256], ("Ufn",), ("ug", 0))
        for gq in range(6):
            sl = gq % 2
            if gq + 1 < 6:
                dma("act", UG[1 - sl], Ufn_v[:, :, 256 * (gq + 1):256 * (gq + 1) + 256], ("Ufn",), ("ug", 1 - sl))
            for nt in range(32):
                bk = pb()
                sch.add("pe", lambda e, bk=bk, sl=sl, nt=nt: e.matmul(psf[bk][:, :], lhsT=FA, rhs=UG[sl][:, 2 * nt:2 * nt + 2, :], start=True, stop=True),
                        r=("fa", ("ug", sl)), w=(("ps", bk),))
                dst = YS[sl][:, 2 * nt:2 * nt + 2, :].rearrange("p a c -> p (a c)")
                if nt % 2 == 0:
                    sch.add("act", lambda e, bk=bk, dst=dst: e.activation(out=dst, in_=psf[bk][:, :], func=AF.Copy), r=(("ps", bk),), w=(("ys", sl, nt),))
                else:
                    sch.add("dve", lambda e, bk=bk, dst=dst: e.tensor_copy(dst, psf[bk][:, :]), r=(("ps", bk),), w=(("ys", sl, nt),))
            dma("sp", Yd[:, :, 256 * gq:256 * gq + 256], YS[sl], tuple(("ys", sl, nt) for nt in range(32)), "Yd")
        barrier()
        F2 = Region(big, F_.off - 4 * 64 * 256 * 2)
        XB = F2.alloc([2, 2, OWN], BF16)
        YB = [F2.alloc([2, 256], BF16) for _ in range(4)]
        ZST = [F2.alloc([512], BF16) for _ in range(2)]
        Yd_v = Yd.rearrange("(r k) s c -> r k s c", r=2)
        zi = 0
        def yb_load(n):
            gq_, j_ = n // 32, n % 32
            dma("act", YB[n % 4], Yd_v[:, 2 * j_:2 * j_ + 2, :, 256 * gq_:256 * gq_ + 256].rearrange("r b s c -> (b s) r c"), ("Yd",), ("yb", n % 4))

        for n in range(3):
            yb_load(n)
        for gq in range(6):
            for j in range(32):
                n_it = 32 * gq + j
                ysl = n_it % 4
                if n_it + 3 < 192:
                    yb_load(n_it + 3)
                t0 = (2 * j) % 8
                x0 = (2 * j) // 8
                for cb in range(2):
                    bk = pb()
                    for ri in range(2):
                        sch.add("pe", lambda e, bk=bk, ysl=ysl, cb=cb, ri=ri, j=j: e.matmul(psf[bk][:, 0:256], lhsT=YB[ysl][:, ri, 128 * cb:128 * cb + 128],
                                                                                             rhs=FG[:, j, ri, :], start=(ri == 0), stop=(ri == 1)),
                                r=(("yb", ysl), "fg"), w=(("ps", bk),))
                    for ro in range(2):
                        src = psf[bk][:, 128 * ro:128 * ro + 128].rearrange("p (b T k) -> p T b k", b=2, T=8)
                        dst = XB[:, cb, ro, :].rearrange("p (T t k x) -> p T t k x", T=8, t=8, k=8)[:, :, t0:t0 + 2, :, x0]
                        if cb == 0:
                            sch.add("act", lambda e, src=src, dst=dst: e.activation(out=dst, in_=src, func=AF.Copy), r=(("ps", bk),), w=(("x", cb, ro, j),))
                        else:
                            sch.add("dve", lambda e, src=src, dst=dst: e.tensor_copy(dst, src), r=(("ps", bk),), w=(("x", cb, ro, j),))
            for kb in range(2):
                for lt in range(8):
                    bk = pb()
                    n_ = 0
                    for cb in range(2):
                        for ri in range(2):
                            sch.add("pe", lambda e, bk=bk, cb=cb, ri=ri, kb=kb, lt=lt, n_=n_: e.matmul(psf[bk][:, :], lhsT=FC[:, cb, ri, 128 * kb:128 * kb + 128],
                                                                                                       rhs=XB[:, cb, ri, 512 * lt:512 * lt + 512],
                                                                                                       start=(n_ == 0), stop=(n_ == 3)),
                                    r=("fc",) + tuple(("x", cb, ri, j) for j in range(32)), w=(("ps", bk),))
                            n_ += 1
                    zs = zi % 2
                    zi += 1
                    if zs == 0:
                        sch.add("act", lambda e, bk=bk, zs=zs: e.activation(out=ZST[zs], in_=psf[bk][:, :], func=AF.Copy), r=(("ps", bk),), w=(("zst", zs),))
                    else:
                        sch.add("dve", lambda e, bk=bk, zs=zs: e.tensor_copy(ZST[zs], psf[bk][:, :]), r=(("ps", bk),), w=(("zst", zs),))
                    row0 = 256 * gq + 128 * kb
                    dma("sp", Yfn[row0:row0 + 128, 512 * lt:512 * lt + 512], ZST[zs], (("zst", zs),), "Yfn")
        barrier()

    if "p4" in phases:
        Q_ = RG(0)
        HT4 = Q_.alloc([16, 512], BF16)
        SCR = Q_.alloc([44, 512], BF16)
        X1 = Q_.alloc([4, 2048], F32)
        NSL = 5
        WS = [Q_.alloc([4096], BF16) for _ in range(NSL)]
        XN4 = [Q_.alloc([2048], BF16) for _ in range(4)]
        GM = Q_.alloc([2048], F32)
        GF = Q_.alloc([2048], F32)
        NF = Q_.alloc([2048], F32)
        TG = [Q_.alloc([512], F32) for _ in range(4)]
        dma("sp", GM, modrow[0:1, :].partition_broadcast(128), ("modrow",), "gm")
        dma("sp", GF, modrow[1:2, :].partition_broadcast(128), ("modrow",), "gf")
        dma("sp", NF, nfin_row[0:1, :].partition_broadcast(128), (), "nf")
        ring = [0]

        def slab(W, wkey, k0, kn, c0, ncols):
            sl = ring[0] % NSL
            ring[0] += 1
            v = WS[sl][:, 0:kn * ncols].rearrange("p (k c) -> p k c", c=ncols)
            dma("sp", v, W.rearrange("(k p) n -> p k n", p=128)[:, k0:k0 + kn, c0:c0 + ncols], (wkey,), ("ws", sl))
            return v, ("ws", sl)

        xs_own = xs.rearrange("(T j1 hf j0 t) d -> T hf t j1 j0 d", T=8, j1=8, hf=2, j0=8, t=8)
        tgc = [0]
        ec4 = [0]
        def p4_first(T_, rs_):
            for r in rs_:
                for tb in range(2):
                    dma("pool", X1[64 * tb:64 * tb + 64, r, :], xs_own[T_, 0, 2 * r + tb], (), ("x1", r))
                rms_block(X1[:, r, :], ("x1", r), XN4[r], ("xn4", r), 8 + r)

        def p4_final(T_, rs_):
            for r in rs_:
                c1 = slice(8 + r, 9 + r)
                sch.add("act", lambda e, r=r, c1=c1: e.activation(out=JUNK, in_=X1[:, r, :], func=AF.Square, accum_out=SS[:, c1]), r=(("x1", r),), w=("junk", ("ss", 8 + r)))
                sch.add("act", lambda e, c1=c1: e.activation(out=SD[:, c1], in_=SS[:, c1], func=AF.Sqrt, scale=1.0 / D, bias=EPSB[:, 0:1]), r=(("ss", 8 + r), "epsb"), w=(("sd", 8 + r),))
                sch.add("dve", lambda e, c1=c1: e.reciprocal(RS[:, c1], SD[:, c1]), r=(("sd", 8 + r),), w=(("rs", 8 + r),))
                sch.add("dve", lambda e, r=r, c1=c1: e.scalar_tensor_tensor(out=X1[:, r, :], in0=X1[:, r, :], scalar=RS[:, c1], in1=NF, op0=ALU.mult, op1=ALU.mult),
                        r=(("x1", r), ("rs", 8 + r), "nf"), w=(("x1", r),))
                dma("pool", yo[512 * T_ + 128 * r:512 * T_ + 128 * r + 128, :], X1[:, r, :], (("x1", r),), "yo")

        for T in range(8):
            if T == 0:
                p4_first(0, (0, 1, 2, 3))
            for r in range(4):
                transpose_mod(XN4[r], ("xn4", r), HT4, ("ht4", r), r, A1, B1, "a1", "modt", ec4)
            hk = tuple(("ht4", r) for r in range(4))
            dma_multi("pool", SCR[:, 16:20, :], Yg.rearrange("(k p) n -> p k n", p=128)[:, :, 512 * T:512 * T + 512], ("Yg",),
                      [("scr", c_) for c_ in range(16, 20)], "scr_yg")
            dma_multi("pool", SCR[:, 24:36, :], Yfn.rearrange("(k p) n -> p k n", p=128)[:, :, 512 * T:512 * T + 512], ("Yfn",),
                      [("scr", c_) for c_ in range(24, 36)], "scr_yfn")
            wv, wk = slab(wb_glu, "wb_glu", 0, 4, 0, 1024)
            for cbk in range(4):
                bv = pb()
                bg = pb()
                for kc in range(4):
                    sch.add("pe", lambda e, bv=bv, kc=kc, cbk=cbk, wv=wv: e.matmul(psf[bv][:, :], lhsT=wv[:, kc, 128 * cbk:128 * cbk + 128], rhs=SCR[:, 16 + kc, :],
                                                                                   start=(kc == 0), stop=(kc == 3)), r=(wk, ("scr", 16 + kc)), w=(("ps", bv),))
                for kc in range(4):
                    sch.add("pe", lambda e, bg=bg, kc=kc, cbk=cbk, wv=wv: e.matmul(psf[bg][:, :], lhsT=wv[:, kc, 512 + 128 * cbk:640 + 128 * cbk], rhs=SCR[:, 16 + kc, :],
                                                                                   start=(kc == 0), stop=(kc == 3)), r=(wk, ("scr", 16 + kc)), w=(("ps", bg),))
                tg = tgc[0] % 4
                tgc[0] += 1
                sch.add("act", lambda e, bg=bg, tg=tg: e.activation(out=TG[tg], in_=psf[bg][:, :], func=AF.Sigmoid), r=(("ps", bg),), w=(("tg", tg),))
                sch.add("dve", lambda e, bv=bv, tg=tg, cbk=cbk: e.tensor_tensor(out=SCR[:, 20 + cbk, :], in0=psf[bv][:, :], in1=TG[tg], op=ALU.mult),
                        r=(("ps", bv), ("tg", tg)), w=(("scr", 20 + cbk),))
            ysk = tuple(("scr", 20 + c_) for c_ in range(4))
            for hq in range(8):
                ga, gak = slab(wb_gate, "wb_gate", 0, 16, 256 * hq, 256)
                ws_, wsk = slab(wb_bs, "wb_bs", 0, 4, 256 * hq, 256)
                gb, gbk = slab(wb_gate, "wb_gate", 0, 16, 2048 + 256 * hq, 256)
                wf_, wfk = slab(wb_bf, "wb_bf", 0, 12, 256 * hq, 256)
                for f2 in range(2):
                    fb = 2 * hq + f2
                    cs = slice(128 * f2, 128 * f2 + 128)
                    b1, b2, b3, b4 = pb(), pb(), pb(), pb()
                    for kc in range(16):
                        sch.add("pe", lambda e, b1=b1, kc=kc, cs=cs, ga=ga: e.matmul(psf[b1][:, :], lhsT=ga[:, kc, cs], rhs=HT4[:, kc, :], start=(kc == 0), stop=(kc == 15)),
                                r=(gak,) + hk, w=(("ps", b1),))
                    for kc in range(4):
                        sch.add("pe", lambda e, b2=b2, kc=kc, cs=cs, ws_=ws_: e.matmul(psf[b2][:, :], lhsT=ws_[:, kc, cs], rhs=SCR[:, 20 + kc, :], start=(kc == 0), stop=(kc == 3)),
                                r=(wsk,) + ysk, w=(("ps", b2),))
                    for kc in range(16):
                        sch.add("pe", lambda e, b3=b3, kc=kc, cs=cs, gb=gb: e.matmul(psf[b3][:, :], lhsT=gb[:, kc, cs], rhs=HT4[:, kc, :], start=(kc == 0), stop=(kc == 15)),
                                r=(gbk,) + hk, w=(("ps", b3),))
                    for kc in range(12):
                        sch.add("pe", lambda e, b4=b4, kc=kc, cs=cs, wf_=wf_: e.matmul(psf[b4][:, :], lhsT=wf_[:, kc, cs], rhs=SCR[:, 24 + kc, :], start=(kc == 0), stop=(kc == 11)),
                                r=(wfk, ("scr", 24 + kc)), w=(("ps", b4),))
                    ta = 2 * (fb % 2)
                    tb = ta + 1
                    sch.add("act", lambda e, b1=b1, ta=ta: e.activation(out=TG[ta], in_=psf[b1][:, :], func=AF.Sigmoid), r=(("ps", b1),), w=(("tg", ta),))
                    sch.add("act", lambda e, b3=b3, tb=tb: e.activation(out=TG[tb], in_=psf[b3][:, :], func=AF.Sigmoid), r=(("ps", b3),), w=(("tg", tb),))
                    sch.add("dve", lambda e, b2=b2, ta=ta: e.tensor_tensor(out=TG[ta], in0=psf[b2][:, :], in1=TG[ta], op=ALU.mult), r=(("ps", b2), ("tg", ta)), w=(("tg", ta),))
                    sch.add("dve", lambda e, b4=b4, tb=tb: e.tensor_tensor(out=TG[tb], in0=psf[b4][:, :], in1=TG[tb], op=ALU.mult), r=(("ps", b4), ("tg", tb)), w=(("tg", tb),))
                    sch.add("dve", lambda e, fb=fb, ta=ta, tb=tb: e.tensor_tensor(out=SCR[:, fb, :], in0=TG[ta], in1=TG[tb], op=ALU.add), r=(("tg", ta), ("tg", tb)), w=(("scr", fb),))
            mgk = tuple(("scr", fb) for fb in range(16))
            for nb in range(4):
                bks = [pb() for _ in range(4)]
                for kg in range(2):
                    wo, wok = slab(wb_out, "wb_out", 8 * kg, 8, 512 * nb, 512)
                    for r in range(4):
                        for k_ in range(8):
                            kc = 8 * kg + k_
                            sch.add("pe", lambda e, bk=bks[r], kc=kc, k_=k_, r=r, wo=wo: e.matmul(psf[bk][:, :], lhsT=SCR[:, kc, 128 * r:128 * r + 128], rhs=wo[:, k_, :],
                                                                                                   start=(kc == 0), stop=(kc == 15)), r=(wok,) + mgk, w=(("ps", bks[r]),))
                cs = slice(512 * nb, 512 * nb + 512)
                for r in range(4):
                    tq = r % 4
                    sch.add("dve", lambda e, bk=bks[r], cs=cs, tq=tq: e.tensor_tensor(out=TG[tq], in0=psf[bk][:, :], in1=GM[:, cs], op=ALU.mult), r=(("ps", bks[r]), "gm"), w=(("tg", tq),))
                    sch.add("dve", lambda e, r=r, cs=cs, tq=tq: e.tensor_tensor(out=X1[:, r, cs], in0=X1[:, r, cs], in1=TG[tq], op=ALU.add), r=(("tg", tq), ("x1", r)), w=(("x1", r),))
            if "dbg_x1" in dbg:
                for r in range(4):
                    dma("sp", dbg_x1[512 * T + 128 * r:512 * T + 128 * r + 128, :], X1[:, r, :], (("x1", r),), "dbg_x1")
            if "dbg_mg" in dbg:
                dma("sp", dbg_mg[:, :, 512 * T:512 * T + 512], SCR[:, 0:24, :], mgk + ysk + tuple(("scr", c_) for c_ in range(16, 20)), "dbg_mg")
            for r in range(4):
                rms_block(X1[:, r, :], ("x1", r), XN4[r], ("xn4", r), 12 + r)
            for r in range(4):
                transpose_mod(XN4[r], ("xn4", r), HT4, ("ht4", r), r, A2, B2, "a2", "modt", ec4)
            for hq in range(22):
                wa, wak = slab(wb_f1, "wb_f1", 0, 16, 256 * hq, 256)
                wb_, wbk = slab(wb_f1, "wb_f1", 0, 16, DFF + 256 * hq, 256)
                for f2 in range(2):
                    cb = 2 * hq + f2
                    cs = slice(128 * f2, 128 * f2 + 128)
                    ba, bb = pb(), pb()
                    for kc in range(16):
                        sch.add("pe", lambda e, ba=ba, kc=kc, cs=cs, wa=wa: e.matmul(psf[ba][:, :], lhsT=wa[:, kc, cs], rhs=HT4[:, kc, :], start=(kc == 0), stop=(kc == 15)),
                                r=(wak,) + hk, w=(("ps", ba),))
                    for kc in range(16):
                        sch.add("pe", lambda e, bb=bb, kc=kc, cs=cs, wb_=wb_: e.matmul(psf[bb][:, :], lhsT=wb_[:, kc, cs], rhs=HT4[:, kc, :], start=(kc == 0), stop=(kc == 15)),
                                r=(wbk,) + hk, w=(("ps", bb),))
                    tg = tgc[0] % 4
                    tgc[0] += 1
                    sch.add("act", lambda e, ba=ba, tg=tg: e.activation(out=TG[tg], in_=psf[ba][:, :], func=AF.Silu), r=(("ps", ba),), w=(("tg", tg),))
                    sch.add("dve", lambda e, bb=bb, tg=tg, cb=cb: e.tensor_tensor(out=SCR[:, cb, :], in0=psf[bb][:, :], in1=TG[tg], op=ALU.mult),
                            r=(("ps", bb), ("tg", tg)), w=(("scr", cb),))
            ack = tuple(("scr", cb) for cb in range(44))
            for rp in range(2):
                rs_ = (2 * rp, 2 * rp + 1)
                for nb in range(4):
                    bks = {r: pb() for r in rs_}
                    for kg in range(6):
                        kn = 8 if kg < 5 else 4
                        w2, w2k = slab(wb_f2, "wb_f2", 8 * kg, kn, 512 * nb, 512)
                        for r in rs_:
                            for k_ in range(kn):
                                kc = 8 * kg + k_
                                sch.add("pe", lambda e, bk=bks[r], kc=kc, k_=k_, r=r, w2=w2: e.matmul(psf[bk][:, :], lhsT=SCR[:, kc, 128 * r:128 * r + 128], rhs=w2[:, k_, :],
                                                                                                       start=(kc == 0), stop=(kc == 43)), r=(w2k,) + ack, w=(("ps", bks[r]),))
                    cs = slice(512 * nb, 512 * nb + 512)
                    for r in rs_:
                        tq = r % 4
                        sch.add("dve", lambda e, bk=bks[r], cs=cs, tq=tq: e.tensor_tensor(out=TG[tq], in0=psf[bk][:, :], in1=GF[:, cs], op=ALU.mult), r=(("ps", bks[r]), "gf"), w=(("tg", tq),))
                        sch.add("dve", lambda e, r=r, cs=cs, tq=tq: e.tensor_tensor(out=X1[:, r, cs], in0=X1[:, r, cs], in1=TG[tq], op=ALU.add), r=(("tg", tq), ("x1", r)), w=(("x1", r),))
                p4_final(T, rs_)
                if T + 1 < 8:
                    p4_first(T + 1, rs_)
    return nc, sch, stack, locals()


def kernel(**inputs):
    inp = {k: np.asarray(v) for k, v in inputs.items()}
    nc, sch, stack, _ = build_nc()
    sch.add("sp", None, r=("yo",))
    sch.emit(nc, stack)
    stack.close()
    in_maps = []
    for core in range(8):
        b, e = core // 2, core % 2
        in_maps.append(_prep_core(inp, b, e, _const_tables(e)))
    res = run_bass_kernel_spmd(nc, in_maps, core_ids=list(range(8)))
    out = np.zeros((4, S, D), np.float32)
    pos = np.arange(OWN)
    T = pos // 512
    q = pos % 512
    t = q // 64
    jj = q % 64
    l = 512 * T + 8 * jj + t
    kp = 128 * (l // 64) + (l % 64)
    for core in range(8):
        b, e = core // 2, core % 2
        act = kp if e == 0 else (S - 1 - kp)
        out[b, act] = res.results[core]["yo"]
    return out
```
